# Optimizing a Trainium2 kernel written in Bass

```python
import math
import jax, jax.numpy as jnp
from jax import lax
import numpy as np

D_MODEL = 1024
BATCH = 16
SEQ = 4096
DEPTH = 4

N_MIXERS = 3
NORM_EPS = 1e-6
CONV_K = 4

MAMBA_D_INNER = 2 * D_MODEL
MAMBA_HEADDIM = 64
MAMBA_HEADS = MAMBA_D_INNER // MAMBA_HEADDIM
MAMBA_GROUPS = 8
MAMBA_D_STATE = 128
MAMBA_CHUNK = 256
MAMBA_CONV_CH = MAMBA_D_INNER + 2 * MAMBA_GROUPS * MAMBA_D_STATE
MAMBA_IN = MAMBA_D_INNER + MAMBA_CONV_CH + MAMBA_HEADS

S5_GROUP_SIZE = 16
S5_GROUPS = D_MODEL // S5_GROUP_SIZE
S5_STATE = 64
S5_CHUNK = 256

GDN_HEADS = D_MODEL // 128
GDN_DK = 128
GDN_DV = 256
GDN_CHUNK = 64
GDN_CONV_CH = GDN_HEADS * (2 * GDN_DK + GDN_DV)
GDN_IN = GDN_CONV_CH + GDN_HEADS * GDN_DV + 2 * GDN_HEADS

FFN_HIDDEN = (8 * D_MODEL + 3 * 256 - 1) // (3 * 256) * 256

N_MAMBA = (DEPTH + 2) // 3
N_S5 = (DEPTH + 1) // 3
N_GDN = DEPTH // 3

kernel_name = 'hybrid_mamba2_s5_gdn_trunk'


def rmsnorm(x, w):
    xf = x.astype(jnp.float32)
    y = xf * lax.rsqrt(jnp.mean(xf * xf, axis=-1, keepdims=True) + NORM_EPS)
    return (y * w.astype(jnp.float32)).astype(x.dtype)


def causal_dwconv(x, w, b=None):
    k, c = w.shape
    out = lax.conv_general_dilated(x, w[:, None, :].astype(x.dtype), window_strides=(1,),
                                   padding=[(k - 1, 0)], dimension_numbers=('NWC', 'WIO', 'NWC'),
                                   feature_group_count=c)
    if b is not None:
        out = out + b.astype(x.dtype)
    return out


def to_chunks(t, chunk):
    b, l = t.shape[:2]
    return jnp.moveaxis(t.reshape(b, l // chunk, chunk, *t.shape[2:]), 1, 0)


def from_chunks(t):
    t = jnp.moveaxis(t, 0, 1)
    return t.reshape(t.shape[0], t.shape[1] * t.shape[2], *t.shape[3:])


def l2norm(t):
    return t * lax.rsqrt(jnp.sum(t * t, axis=-1, keepdims=True) + 1e-6)


def ssd_scan(x, da, bm, cm):
    bsz, l, h, p = x.shape
    g, n = bm.shape[2:]
    r = h // g
    chunk = math.gcd(l, MAMBA_CHUNK)
    xc = to_chunks(x.reshape(bsz, l, g, r, p), chunk)
    ac = to_chunks(da.reshape(bsz, l, g, r), chunk)
    bc = to_chunks(bm, chunk)
    cc = to_chunks(cm, chunk)
    causal = jnp.tril(jnp.ones((chunk, chunk), dtype=bool))[None, :, :, None, None]

    def step(state, inp):
        xk, ak, bk, ck = inp
        a_cs = jnp.cumsum(ak, axis=1)
        seg = a_cs[:, :, None] - a_cs[:, None, :]
        decay = jnp.exp(jnp.where(causal, seg, -jnp.inf))
        cb = jnp.einsum('blgn,bsgn->blsg', ck, bk)
        y_diag = jnp.einsum('blsgr,bsgrp->blgrp', cb[..., None] * decay, xk)
        y_off = jnp.einsum('blgn,bgrpn->blgrp', ck, state) * jnp.exp(a_cs)[..., None]
        to_end = jnp.exp(a_cs[:, -1:] - a_cs)
        new_state = (state * jnp.exp(a_cs[:, -1])[..., None, None]
                     + jnp.einsum('bsgn,bsgrp->bgrpn', bk, xk * to_end[..., None]))
        return new_state, y_diag + y_off

    state0 = jnp.zeros((bsz, g, r, p, n), jnp.float32)
    _, ys = lax.scan(step, state0, (xc, ac, bc, cc))
    return from_chunks(ys).reshape(bsz, l, h, p)


def mamba2_mixer(h, w_in, conv_w, conv_b, dt_bias, a_log, d_skip, norm_w, w_out):
    bsz, l, _ = h.shape
    f32 = jnp.float32
    proj = h @ w_in
    z = proj[..., :MAMBA_D_INNER]
    xbc = proj[..., MAMBA_D_INNER:MAMBA_D_INNER + MAMBA_CONV_CH]
    dt = proj[..., MAMBA_D_INNER + MAMBA_CONV_CH:]
    xbc = jax.nn.silu(causal_dwconv(xbc, conv_w, conv_b)).astype(f32)
    gn = MAMBA_GROUPS * MAMBA_D_STATE
    xs = xbc[..., :MAMBA_D_INNER].reshape(bsz, l, MAMBA_HEADS, MAMBA_HEADDIM)
    bm = xbc[..., MAMBA_D_INNER:MAMBA_D_INNER + gn].reshape(bsz, l, MAMBA_GROUPS, MAMBA_D_STATE)
    cm = xbc[..., MAMBA_D_INNER + gn:].reshape(bsz, l, MAMBA_GROUPS, MAMBA_D_STATE)
    dt = jax.nn.softplus(dt.astype(f32) + dt_bias.astype(f32))
    a = -jnp.exp(a_log.astype(f32))
    y = ssd_scan(xs * dt[..., None], dt * a, bm, cm) + d_skip.astype(f32)[:, None] * xs
    gs = MAMBA_D_INNER // MAMBA_GROUPS
    y = y.reshape(bsz, l, MAMBA_GROUPS, gs) * jax.nn.silu(z.astype(f32)).reshape(bsz, l, MAMBA_GROUPS, gs)
    y = y * lax.rsqrt(jnp.mean(y * y, axis=-1, keepdims=True) + NORM_EPS)
    y = y.reshape(bsz, l, MAMBA_D_INNER) * norm_w.astype(f32)
    return y.astype(h.dtype) @ w_out


def s5_mixer(h, lam_re, lam_im, log_dt, b_re, b_im, c_re, c_im, d_skip, w_glu, b_glu):
    bsz, l, d = h.shape
    f32 = jnp.float32
    lr, li = lam_re.astype(f32), lam_im.astype(f32)
    b_re, b_im = b_re.astype(f32), b_im.astype(f32)
    c_re, c_im = c_re.astype(f32), c_im.astype(f32)
    dt = jnp.exp(log_dt.astype(f32))[:, None]
    mag = jnp.exp(lr * dt)
    lbar_re, lbar_im = mag * jnp.cos(li * dt), mag * jnp.sin(li * dt)
    den = lr * lr + li * li
    zr = ((lbar_re - 1.0) * lr + lbar_im * li) / den
    zi = (lbar_im * lr - (lbar_re - 1.0) * li) / den
    bb_re = zr[..., None] * b_re - zi[..., None] * b_im
    bb_im = zr[..., None] * b_im + zi[..., None] * b_re
    chunk = math.gcd(l, S5_CHUNK)
    u = h.astype(f32).reshape(bsz, l, S5_GROUPS, S5_GROUP_SIZE)
    uc = to_chunks(u, chunk)

    def combine(e1, e2):
        a1r, a1i, b1r, b1i = e1
        a2r, a2i, b2r, b2i = e2
        return (a2r * a1r - a2i * a1i, a2r * a1i + a2i * a1r,
                a2r * b1r - a2i * b1i + b2r, a2r * b1i + a2i * b1r + b2i)

    def step(carry, uk):
        s0r, s0i = carry
        bur = jnp.einsum('bcgi,gpi->bcgp', uk, bb_re)
        bui = jnp.einsum('bcgi,gpi->bcgp', uk, bb_im)
        ar = jnp.broadcast_to(lbar_re, bur.shape)
        ai = jnp.broadcast_to(lbar_im, bur.shape)
        pr, pim, sr, si = lax.associative_scan(combine, (ar, ai, bur, bui), axis=1)
        sr, si = (sr + pr * s0r[:, None] - pim * s0i[:, None],
                  si + pr * s0i[:, None] + pim * s0r[:, None])
        y = jnp.einsum('bcgp,gip->bcgi', sr, c_re) - jnp.einsum('bcgp,gip->bcgi', si, c_im)
        return (sr[:, -1], si[:, -1]), y

    carry0 = (jnp.zeros((bsz, S5_GROUPS, S5_STATE), f32), jnp.zeros((bsz, S5_GROUPS, S5_STATE), f32))
    _, ys = lax.scan(step, carry0, uc)
    y = from_chunks(ys).reshape(bsz, l, d) + d_skip.astype(f32) * h.astype(f32)
    g = jax.nn.gelu(y).astype(h.dtype)
    gl = g @ w_glu + b_glu
    return gl[..., :d] * jax.nn.sigmoid(gl[..., d:])


def gated_delta_rule(q, k, v, g, beta):
    bsz, l, hh, dk = q.shape
    dv = v.shape[-1]
    chunk = math.gcd(l, GDN_CHUNK)

    def heads_first(t):
        return jnp.swapaxes(to_chunks(t, chunk), 2, 3)

    causal = jnp.tril(jnp.ones((chunk, chunk), dtype=bool))
    strict = jnp.tril(jnp.ones((chunk, chunk), dtype=bool), -1)
    eye = jnp.eye(chunk, dtype=jnp.float32)

    def step(s, inp):
        qc, kc, vc, gk, bk = inp
        gc = jnp.cumsum(gk, axis=-1)
        decay = jnp.exp(jnp.where(causal, gc[..., :, None] - gc[..., None, :], -jnp.inf))
        kb = kc * bk[..., None]
        m = jnp.where(strict, jnp.einsum('bhid,bhjd->bhij', kb, kc) * decay, 0.0)
        rhs = jnp.concatenate([vc * bk[..., None], kb * jnp.exp(gc)[..., None]], axis=-1)
        sol = lax.linalg.triangular_solve(eye + m, rhs, left_side=True, lower=True, unit_diagonal=True)
        u, w = sol[..., :dv], sol[..., dv:]
        v_new = u - jnp.einsum('bhck,bhkv->bhcv', w, s)
        attn = jnp.einsum('bhik,bhjk->bhij', qc, kc) * decay
        o = (jnp.einsum('bhck,bhkv->bhcv', qc * jnp.exp(gc)[..., None], s)
             + jnp.einsum('bhij,bhjv->bhiv', attn, v_new))
        s = (s * jnp.exp(gc[..., -1])[..., None, None]
             + jnp.einsum('bhck,bhcv->bhkv', kc * jnp.exp(gc[..., -1:] - gc)[..., None], v_new))
        return s, o

    s0 = jnp.zeros((bsz, hh, dk, dv), jnp.float32)
    _, os_ = lax.scan(step, s0, (heads_first(q), heads_first(k), heads_first(v), heads_first(g), heads_first(beta)))
    return from_chunks(jnp.swapaxes(os_, 2, 3))


def gated_deltanet_mixer(h, w_in, conv_w, a_log, dt_bias, norm_w, w_out):
    bsz, l, _ = h.shape
    f32 = jnp.float32
    proj = h @ w_in
    qkv = jax.nn.silu(causal_dwconv(proj[..., :GDN_CONV_CH], conv_w)).astype(f32)
    kd = GDN_HEADS * GDN_DK
    q = l2norm(qkv[..., :kd].reshape(bsz, l, GDN_HEADS, GDN_DK)) * (GDN_DK ** -0.5)
    k = l2norm(qkv[..., kd:2 * kd].reshape(bsz, l, GDN_HEADS, GDN_DK))
    v = qkv[..., 2 * kd:].reshape(bsz, l, GDN_HEADS, GDN_DV)
    off = GDN_CONV_CH + GDN_HEADS * GDN_DV
    gate = proj[..., GDN_CONV_CH:off].astype(f32).reshape(bsz, l, GDN_HEADS, GDN_DV)
    beta = jax.nn.sigmoid(proj[..., off:off + GDN_HEADS].astype(f32))
    a = proj[..., off + GDN_HEADS:].astype(f32)
    g = -jnp.exp(a_log.astype(f32)) * jax.nn.softplus(a + dt_bias.astype(f32))
    o = gated_delta_rule(q, k, v, g, beta)
    o = o * lax.rsqrt(jnp.mean(o * o, axis=-1, keepdims=True) + NORM_EPS) * norm_w.astype(f32) * jax.nn.silu(gate)
    return o.reshape(bsz, l, GDN_HEADS * GDN_DV).astype(h.dtype) @ w_out


def swiglu(h, w_in, w_out):
    gu = h @ w_in
    return (jax.nn.silu(gu[..., :FFN_HIDDEN]) * gu[..., FFN_HIDDEN:]) @ w_out


def setup_inputs(seed: int = 0) -> dict:
    key = jax.random.key(seed)
    ks = iter(jax.random.split(key, 40))
    f32 = jnp.float32

    def nrm(shape, scale):
        return jax.random.normal(next(ks), shape, f32) * scale

    def unif(shape, lo, hi):
        return jax.random.uniform(next(ks), shape, f32, lo, hi)

    def dt_bias_init(shape):
        dt = jnp.exp(unif(shape, math.log(1e-3), math.log(1e-1)))
        return dt + jnp.log(-jnp.expm1(-dt))

    d = D_MODEL
    return {
        'x': nrm((BATCH, SEQ, d), 1.0),
        'mix_norm_w': 1.0 + nrm((DEPTH, d), 0.02),
        'mamba_w_in': nrm((N_MAMBA, d, MAMBA_IN), d ** -0.5),
        'mamba_conv_w': nrm((N_MAMBA, CONV_K, MAMBA_CONV_CH), CONV_K ** -0.5),
        'mamba_conv_b': nrm((N_MAMBA, MAMBA_CONV_CH), 0.02),
        'mamba_dt_bias': dt_bias_init((N_MAMBA, MAMBA_HEADS)),
        'mamba_a_log': jnp.log(unif((N_MAMBA, MAMBA_HEADS), 1.0, 16.0)),
        'mamba_d': 1.0 + nrm((N_MAMBA, MAMBA_HEADS), 0.02),
        'mamba_norm_w': 1.0 + nrm((N_MAMBA, MAMBA_D_INNER), 0.02),
        'mamba_w_out': nrm((N_MAMBA, MAMBA_D_INNER, d), MAMBA_D_INNER ** -0.5),
        's5_lam_re': -0.5 + nrm((N_S5, S5_GROUPS, S5_STATE), 0.01),
        's5_lam_im': jnp.broadcast_to(math.pi * jnp.arange(S5_STATE, dtype=f32), (N_S5, S5_GROUPS, S5_STATE)),
        's5_log_dt': unif((N_S5, S5_GROUPS), math.log(1e-3), math.log(1e-1)),
        's5_b_re': nrm((N_S5, S5_GROUPS, S5_STATE, S5_GROUP_SIZE), (2 * S5_GROUP_SIZE) ** -0.5),
        's5_b_im': nrm((N_S5, S5_GROUPS, S5_STATE, S5_GROUP_SIZE), (2 * S5_GROUP_SIZE) ** -0.5),
        's5_c_re': nrm((N_S5, S5_GROUPS, S5_GROUP_SIZE, S5_STATE), (2 * S5_STATE) ** -0.5),
        's5_c_im': nrm((N_S5, S5_GROUPS, S5_GROUP_SIZE, S5_STATE), (2 * S5_STATE) ** -0.5),
        's5_d': nrm((N_S5, d), 1.0),
        's5_w_glu': nrm((N_S5, d, 2 * d), d ** -0.5),
        's5_b_glu': nrm((N_S5, 2 * d), 0.02),
        'gdn_w_in': nrm((N_GDN, d, GDN_IN), d ** -0.5),
        'gdn_conv_w': nrm((N_GDN, CONV_K, GDN_CONV_CH), CONV_K ** -0.5),
        'gdn_a_log': jnp.log(unif((N_GDN, GDN_HEADS), 1.0, 16.0)),
        'gdn_dt_bias': dt_bias_init((N_GDN, GDN_HEADS)),
        'gdn_norm_w': 1.0 + nrm((N_GDN, GDN_DV), 0.02),
        'gdn_w_out': nrm((N_GDN, GDN_HEADS * GDN_DV, d), (GDN_HEADS * GDN_DV) ** -0.5),
        'ffn_norm_w': 1.0 + nrm((DEPTH, d), 0.02),
        'ffn_w_in': nrm((DEPTH, d, 2 * FFN_HIDDEN), d ** -0.5),
        'ffn_w_out': nrm((DEPTH, FFN_HIDDEN, d), FFN_HIDDEN ** -0.5),
        'final_norm_w': 1.0 + nrm((d,), 0.02),
    }


def reference(x, mix_norm_w, mamba_w_in, mamba_conv_w, mamba_conv_b, mamba_dt_bias, mamba_a_log, mamba_d,
              mamba_norm_w, mamba_w_out, s5_lam_re, s5_lam_im, s5_log_dt, s5_b_re, s5_b_im, s5_c_re, s5_c_im,
              s5_d, s5_w_glu, s5_b_glu, gdn_w_in, gdn_conv_w, gdn_a_log, gdn_dt_bias, gdn_norm_w, gdn_w_out,
              ffn_norm_w, ffn_w_in, ffn_w_out, final_norm_w):
    for i in range(DEPTH):
        kind = i % N_MIXERS
        j = i // N_MIXERS
        h = rmsnorm(x, mix_norm_w[i])
        if kind == 0:
            mix = mamba2_mixer(h, mamba_w_in[j], mamba_conv_w[j], mamba_conv_b[j], mamba_dt_bias[j],
                               mamba_a_log[j], mamba_d[j], mamba_norm_w[j], mamba_w_out[j])
        elif kind == 1:
            mix = s5_mixer(h, s5_lam_re[j], s5_lam_im[j], s5_log_dt[j], s5_b_re[j], s5_b_im[j],
                           s5_c_re[j], s5_c_im[j], s5_d[j], s5_w_glu[j], s5_b_glu[j])
        else:
            mix = gated_deltanet_mixer(h, gdn_w_in[j], gdn_conv_w[j], gdn_a_log[j], gdn_dt_bias[j],
                                       gdn_norm_w[j], gdn_w_out[j])
        x = x + mix.astype(x.dtype)
        x = x + swiglu(rmsnorm(x, ffn_norm_w[i]), ffn_w_in[i], ffn_w_out[i]).astype(x.dtype)
    return rmsnorm(x, final_norm_w)
```

```python
import numpy as np
import concourse.bass as bass
import concourse.mybir as mybir
from concourse.bass_utils import run_bass_kernel_spmd

F32 = mybir.dt.float32
BF16 = mybir.dt.bfloat16
AF = mybir.ActivationFunctionType
ALU = mybir.AluOpType

SAME_ENGINE_SYNC = True
N_DMA_SEMS = 40


def _region(ap):
    t = ap.tensor
    pat = ap.ap
    off = ap.offset
    space = str(ap.space)
    if 'DRAM' in space.upper() or 'HBM' in space.upper() or type(t).__name__.startswith('DRam'):
        ext = 1
        for st, cnt in pat:
            ext += (cnt - 1) * abs(st)
        return (t.name, 0, 1, off, off + ext)
    if type(t).__name__.startswith('PSum'):
        return (t.name, 0, 128, 0, 1 << 30)
    pstep, pcnt = pat[0]
    if pstep == 0:
        pstep = 1 << 60
    tshape = t.shape
    per_part = 1
    for s in tshape[1:]:
        per_part *= s
    p0 = off // per_part
    f0 = off % per_part
    ext = 1
    for st, cnt in pat[1:]:
        ext += (cnt - 1) * abs(st)
    return (t.name, p0, p0 + pcnt, f0, f0 + ext)


def _overlap(a, b):
    return a[1] < b[2] and b[1] < a[2] and a[3] < b[4] and b[3] < a[4]


def _contains(a, b):
    return a[1] <= b[1] and a[2] >= b[2] and a[3] <= b[3] and a[4] >= b[4]


class Prog:
    ENG = ('pe', 'dve', 'act', 'pool', 'sp')
    G = {}

    def __init__(self, nc):
        self.nc = nc
        self.ins = []
        self.acc = {}
        self.nbuf = 0
        from contextlib import ExitStack
        self.st = ExitStack()
        Prog.NPROG = getattr(Prog, 'NPROG', 0) + 1
        self.pid = Prog.NPROG

    def sb(self, shape, dtype, name=None):
        self.nbuf += 1
        return self.st.enter_context(self.nc.sbuf_tensor(f"p{self.pid}_" + (name or f"sb{self.nbuf}"), list(shape), dtype))

    def ps(self, shape, dtype=F32, name=None):
        self.nbuf += 1
        return self.nc.alloc_psum_tensor(name or f"ps{self.nbuf}", list(shape), dtype)

    def dram(self, name, shape, dtype, kind="Internal"):
        return self.nc.dram_tensor(name, list(shape), dtype, kind=kind).ap()

    def add(self, eng, fn, reads, writes, dma=False):
        idx = len(self.ins)
        deps = set()
        eid = ('dma', idx) if dma else eng
        psr = [ap for ap in reads if type(ap.tensor).__name__.startswith('PSum')]
        if psr:
            reads = [ap for ap in reads if not type(ap.tensor).__name__.startswith('PSum')]
            writes = list(writes) + psr
        for ap in reads:
            r = _region(ap)
            a = self.acc.setdefault(r[0], {'w': [], 'r': []})
            for (wr, wi, we) in a['w']:
                if _overlap(wr, r):
                    deps.add(wi)
            a['r'] = [(rr, ri, re) for (rr, ri, re) in a['r'] if not (re == eid and _contains(r, rr))]
            a['r'].append((r, idx, eid))
        for ap in writes:
            w = _region(ap)
            a = self.acc.setdefault(w[0], {'w': [], 'r': []})
            for (wr, wi, we) in a['w']:
                if _overlap(wr, w):
                    deps.add(wi)
            for (rr, ri, re) in a['r']:
                if _overlap(rr, w) and ri != idx:
                    deps.add(ri)
            a['w'] = [(wr, wi, we) for (wr, wi, we) in a['w'] if not _contains(w, wr)]
            a['r'] = [(rr, ri, re) for (rr, ri, re) in a['r'] if not (_contains(w, rr) and ri != idx)]
            a['w'].append((w, idx, eid))
        deps.discard(idx)
        self.ins.append(dict(eng=eng, fn=fn, deps=deps, dma=dma))
        return idx

    def finish(self):
        self.emit()
        self.st.close()
        self.ins = []
        self.acc = {}

    def emit(self):
        nc = self.nc
        ins = self.ins
        n = len(ins)
        G = getattr(nc, '_mk_G', None) if False else Prog.G.get(id(nc))
        if G is None:
            G = dict(esem={e: nc.semaphore(f"sg_{e}").__enter__() for e in self.ENG},
                     dsem=[nc.semaphore(f"sg_dma{k}").__enter__() for k in range(N_DMA_SEMS)],
                     cnt={e: 0 for e in self.ENG}, dma_count=[0] * N_DMA_SEMS, dma_next=0)
            Prog.G[id(nc)] = G
        esem, dsem = G['esem'], G['dsem']
        start_cnt = dict(G['cnt'])
        start_dma = list(G['dma_count'])
        need = [False] * n
        for i, it in enumerate(ins):
            for d in it['deps']:
                de = ins[d]
                if de['dma']:
                    need[d] = True
                elif de['eng'] != it['eng'] or it['dma']:
                    need[d] = True
                elif SAME_ENGINE_SYNC and it['eng'] != 'pe':
                    need[d] = True
        last_of = {}
        for i, it in enumerate(ins):
            if not it['dma']:
                last_of[it['eng']] = i
        for e, i in last_of.items():
            need[i] = True
        cnt = G['cnt']
        dma_count = G['dma_count']
        ms = [None] * n
        dma_wait_before = [None] * n
        for i, it in enumerate(ins):
            if it['dma']:
                s_ = G['dma_next'] % N_DMA_SEMS
                G['dma_next'] += 1
                if dma_count[s_] > 0:
                    dma_wait_before[i] = (s_, dma_count[s_])
                dma_count[s_] += 16
                ms[i] = ('dma', s_, dma_count[s_])
            elif need[i]:
                cnt[it['eng']] += 1
                ms[i] = ('eng', it['eng'], cnt[it['eng']])
        end_dma = list(dma_count)
        with nc.Block() as block:

            def run(engname, e):
                known = {}

                def wait_for(m):
                    kind, key, val = m
                    if known.get((kind, key), 0) >= val:
                        return
                    sem = esem[key] if kind == 'eng' else dsem[key]
                    e.wait_ge(sem, val)
                    known[(kind, key)] = val

                for e2 in self.ENG:
                    if start_cnt[e2] > 0 and e2 != engname:
                        wait_for(('eng', e2, start_cnt[e2]))
                for s_ in range(N_DMA_SEMS):
                    if start_dma[s_] > 0:
                        wait_for(('dma', s_, start_dma[s_]))
                if start_cnt[engname] > 0:
                    known[('eng', engname)] = start_cnt[engname]
                for i, it in enumerate(ins):
                    if it['eng'] != engname:
                        continue
                    for d in sorted(it['deps']):
                        de = ins[d]
                        if ms[d] is None:
                            continue
                        if (not de['dma']) and de['eng'] == engname and not it['dma']:
                            if not (SAME_ENGINE_SYNC and engname != 'pe'):
                                continue
                        wait_for(ms[d])
                    if it['dma'] and dma_wait_before[i] is not None:
                        wait_for(('dma',) + dma_wait_before[i])
                    r = it['fn'](e)
                    if ms[i] is not None:
                        kind, key, val = ms[i]
                        if kind == 'dma':
                            r.then_inc(dsem[key], 16)
                        else:
                            r.then_inc(esem[key], 1)
                if engname == 'sp':
                    for s_ in range(N_DMA_SEMS):
                        if end_dma[s_] > 0:
                            wait_for(('dma', s_, end_dma[s_]))

            @block.tensor
            def _(e):
                run('pe', e)

            @block.vector
            def _(e):
                run('dve', e)

            @block.scalar
            def _(e):
                run('act', e)

            @block.gpsimd
            def _(e):
                run('pool', e)

            @block.sync
            def _(e):
                run('sp', e)

    def dma(self, out, in_, eng='sp', **kw):
        return self.add(eng, lambda e: e.dma_start(out=out, in_=in_, **kw), [in_], [out], dma=True)

    def mm(self, out, lhsT, rhs, start=True, stop=True, **kw):
        return self.add('pe', lambda e: e.matmul(out, lhsT, rhs, start=start, stop=stop, **kw), [lhsT, rhs] + ([] if start else [out]), [out])

    def transpose(self, out, in_, ident):
        return self.add('pe', lambda e: e.transpose(out, in_, ident), [in_, ident], [out])

    def act(self, out, in_, func, bias=None, scale=None, eng='act'):
        kw = {}
        rd = [in_]
        if bias is not None:
            kw['bias'] = bias
            if not isinstance(bias, (int, float)):
                rd.append(bias)
        if scale is not None:
            kw['scale'] = scale
            if not isinstance(scale, (int, float)):
                rd.append(scale)
        return self.add(eng, lambda e: e.activation(out=out, in_=in_, func=func, **kw), rd, [out])

    def tt(self, out, in0, in1, op, eng='dve'):
        return self.add(eng, lambda e: e.tensor_tensor(out=out, in0=in0, in1=in1, op=op), [in0, in1], [out])

    def ts(self, out, in0, s1, op0, s2=None, op1=None, eng='dve'):
        rd = [in0]
        if not isinstance(s1, (int, float)):
            rd.append(s1)
        if s2 is not None and not isinstance(s2, (int, float)):
            rd.append(s2)
        if op1 is None:
            return self.add(eng, lambda e: e.tensor_scalar(out=out, in0=in0, scalar1=s1, scalar2=None, op0=op0), rd, [out])
        return self.add(eng, lambda e: e.tensor_scalar(out=out, in0=in0, scalar1=s1, scalar2=s2, op0=op0, op1=op1), rd, [out])

    def stt(self, out, in0, scalar, in1, op0, op1, eng='dve'):
        rd = [in0, in1]
        if not isinstance(scalar, (int, float)):
            rd.append(scalar)
        return self.add(eng, lambda e: e.scalar_tensor_tensor(out=out, in0=in0, scalar=scalar, in1=in1, op0=op0, op1=op1), rd, [out])

    def copy(self, out, in_, eng='dve'):
        if eng == 'act':
            return self.add(eng, lambda e: e.copy(out=out, in_=in_), [in_], [out])
        return self.add(eng, lambda e: e.tensor_copy(out=out, in_=in_), [in_], [out])

    def memset(self, ap, val, eng='dve'):
        return self.add(eng, lambda e: e.memset(ap, val), [], [ap])

    def recip(self, out, in_):
        return self.add('dve', lambda e: e.reciprocal(out=out, in_=in_), [in_], [out])
D = 1024
KC = 8
FH = 2816
EPS = 1e-6


class Rot:
    def __init__(self, items):
        self.items = items
        self.i = 0

    def next(self):
        r = self.items[self.i % len(self.items)]
        self.i += 1
        return r


def make_consts():
    c = {}
    c['ident'] = np.eye(128, dtype=np.float32)
    c['ones'] = np.ones((128, 128), np.float32)
    t = np.arange(128)
    c['ms'] = (t[:, None] > t[None, :]).astype(np.float32)
    c['ule'] = (t[:, None] <= t[None, :]).astype(np.float32)
    c['ones2'] = np.ones((128, 128), np.float32)
    c['negtri'] = np.where(t[None, :] < t[:, None], -30000.0, 0.0).astype(np.float32)
    c['negtri_s'] = np.where(t[None, :] >= t[:, None], -30000.0, 0.0).astype(np.float32)
    c['zeros'] = np.zeros((128, 128), np.float32)
    names = list(c.keys())
    arr = np.concatenate([c[n] for n in names], axis=1)
    offs = {n: i * 128 for i, n in enumerate(names)}
    return arr, offs


class Ctx:
    def __init__(self, P, consts_ap, offs):
        self.P = P
        nc = P.nc
        ncol = consts_ap.shape[1]
        self.c32 = P.nc.alloc_sbuf_tensor("c32", [128, ncol], F32)
        self.c16 = P.nc.alloc_sbuf_tensor("c16", [128, ncol], BF16)
        P.dma(self.c32[:], consts_ap)
        P.copy(self.c16[:], self.c32[:])
        self.offs = offs
        self.banks = Rot([P.ps([128, 512], F32, name=f"bank{i}") for i in range(8)])
        self.epsb = P.nc.alloc_sbuf_tensor("epsb", [128, 1], F32)
        P.memset(self.epsb[:], EPS)
        self.oneb = P.nc.alloc_sbuf_tensor("oneb", [128, 1], F32)
        P.memset(self.oneb[:], 1.0)

    def k32(self, name, p=128, n=128):
        o = self.offs[name]
        return self.c32[0:p, o:o + n]

    def k16(self, name, p=128, n=128):
        o = self.offs[name]
        return self.c16[0:p, o:o + n]

    def bank(self):
        return self.banks.next()


def rstd_from_sumsq(cx, out_rstd, ss_psum, n, tmp):
    P = cx.P
    P.act(tmp, ss_psum, AF.Sqrt, bias=cx.epsb[:], scale=1.0 / n)
    P.recip(out_rstd, tmp)


def rmsnorm_tile(cx, xt, w_sb, h_out, TT, sq, rstd, tmp):
    P = cx.P
    bank = cx.bank()
    for k in range(KC):
        P.act(sq[:, k, 0:TT], xt[:, k, 0:TT], AF.Square)
    for k in range(KC):
        P.mm(bank[:, 0:TT], cx.k16('ones'), sq[:, k, 0:TT], start=(k == 0), stop=(k == KC - 1))
    rstd_from_sumsq(cx, rstd[:, 0:TT], bank[:, 0:TT], D, tmp[:, 0:TT])
    for k in range(KC):
        P.stt(h_out[:, k, 0:TT], xt[:, k, 0:TT], w_sb[:, k:k + 1], rstd[:, 0:TT], ALU.mult, ALU.mult)


def phase_load(cx, x_in, xT, NS, L):
    P = cx.P
    xin = Rot([P.sb([128, D], F32, name=f"ld_in{i}") for i in range(3)])
    xo = Rot([P.sb([128, KC, 512], F32, name=f"ld_o{i}") for i in range(2)])
    xTv = xT.rearrange("(k p) t -> p k t", p=128)
    for s in range(NS):
        for b4 in range(L // 512):
            ot = xo.next()
            for bb in range(4):
                l0 = b4 * 512 + bb * 128
                it = xin.next()
                P.dma(it[:], x_in[s, l0:l0 + 128, :])
                for half in range(2):
                    bank = cx.bank()
                    for kk in range(4):
                        k = half * 4 + kk
                        P.transpose(bank[:, kk * 128:(kk + 1) * 128], it[:, k * 128:(k + 1) * 128], cx.k32('ident'))
                    src = bank[:, :].rearrange("p (k t) -> p k t", k=4)
                    eng = 'act' if half == 0 else 'dve'
                    P.copy(ot[:, half * 4:half * 4 + 4, bb * 128:(bb + 1) * 128], src, eng=eng)
            t0 = s * L + b4 * 512
            P.dma(xTv[:, :, t0:t0 + 512], ot[:])


def phase_final(cx, xT, w_ap, out, NS, L):
    P = cx.P
    TT = 512
    w_sb = P.sb([128, KC], F32, name="fin_w")
    P.dma(w_sb[:], w_ap)
    xt = Rot([P.sb([128, KC, TT], F32, name=f"fin_x{i}") for i in range(2)])
    hs = Rot([P.sb([128, KC, TT], F32, name=f"fin_h{i}") for i in range(2)])
    sq = P.sb([128, KC, TT], BF16, name="fin_sq")
    rstd = P.sb([128, TT], F32, name="fin_rstd")
    tmp = P.sb([128, TT], F32, name="fin_tmp")
    ob = Rot([P.sb([128, D], F32, name=f"fin_o{i}") for i in range(3)])
    xTv = xT.rearrange("(k p) t -> p k t", p=128)
    for s in range(NS):
        for b4 in range(L // TT):
            t0 = s * L + b4 * TT
            x = xt.next()
            h = hs.next()
            P.dma(x[:], xTv[:, :, t0:t0 + TT])
            rmsnorm_tile(cx, x, w_sb, h, TT, sq, rstd, tmp)
            for bb in range(4):
                o = ob.next()
                for half in range(2):
                    bank = cx.bank()
                    for kk in range(4):
                        k = half * 4 + kk
                        P.transpose(bank[:, kk * 128:(kk + 1) * 128], h[:, k, bb * 128:(bb + 1) * 128], cx.k32('ident'))
                    P.copy(o[:, half * 512:(half + 1) * 512], bank[:, :], eng='act' if half == 0 else 'dve')
                l0 = b4 * TT + bb * 128
                P.dma(out[s, l0:l0 + 128, :], o[:])


def load_w_bf16(cx, dst, src_ap, rows, cols, col0=0, ncols=None):
    P = cx.P
    ncols = cols if ncols is None else ncols
    v = src_ap.rearrange("(k p) n -> p k n", p=128)
    kc = rows // 128
    if not hasattr(P, 'wstage'):
        P.wstage = Rot([P.sb([128, 1024], F32, name=f"wstage{i}") for i in range(2)])
    for k in range(kc):
        for c0 in range(0, ncols, 1024):
            c1 = min(ncols, c0 + 1024)
            st = P.wstage.next()
            P.dma(st[:, 0:c1 - c0], v[:, k, col0 + c0:col0 + c1])
            P.copy(dst[:, k, c0:c1], st[:, 0:c1 - c0], eng='pool')


def phase_ffn(cx, xT, norm_w, w_in, w_out, T):
    P = cx.P
    TT = 256
    HC = FH // 128
    win = P.sb([128, KC, 2 * FH], BF16, name="ffn_win")
    wout = P.sb([128, HC, D], BF16, name="ffn_wout")
    load_w_bf16(cx, win, w_in, D, 2 * FH)
    load_w_bf16(cx, wout, w_out, FH, D)
    w_sb = P.sb([128, KC], F32, name="ffn_nw")
    P.dma(w_sb[:], norm_w)
    xt = Rot([P.sb([128, KC, TT], F32, name=f"ffn_x{i}") for i in range(2)])
    hb = Rot([P.sb([128, KC, TT], BF16, name=f"ffn_h{i}") for i in range(2)])
    sq = P.sb([128, KC, TT], BF16, name="ffn_sq")
    rstd = P.sb([128, TT], F32, name="ffn_rstd")
    tmp = P.sb([128, TT], F32, name="ffn_tmp")
    sg = Rot([P.sb([128, TT], F32, name=f"ffn_sg{i}") for i in range(3)])
    actT = Rot([P.sb([128, HC, TT], BF16, name=f"ffn_act{i}") for i in range(1)])
    xTv = xT.rearrange("(k p) t -> p k t", p=128)
    DBG = 9
    for ti in range(T // TT):
        if DBG < 2:
            break
        t0 = ti * TT
        x = xt.next()
        h = hb.next()
        a = actT.next()
        P.dma(x[:], xTv[:, :, t0:t0 + TT])
        rmsnorm_tile(cx, x, w_sb, h, TT, sq, rstd, tmp)
        for c in range(HC):
            if DBG < 3:
                break
            bg = cx.bank()
            bu = cx.bank()
            for k in range(KC):
                P.mm(bg[:, 0:TT], win[:, k, c * 128:(c + 1) * 128], h[:, k, :], start=(k == 0), stop=(k == KC - 1))
            for k in range(KC):
                P.mm(bu[:, 0:TT], win[:, k, FH + c * 128:FH + (c + 1) * 128], h[:, k, :], start=(k == 0), stop=(k == KC - 1))
            s_ = sg.next()
            P.act(s_[:], bg[:, 0:TT], AF.Silu)
            P.tt(a[:, c, :], s_[:], bu[:, 0:TT], ALU.mult)
        for m in range(KC):
            if DBG < 4:
                break
            bo = cx.bank()
            for c in range(HC):
                P.mm(bo[:, 0:TT], wout[:, c, m * 128:(m + 1) * 128], a[:, c, :], start=(c == 0), stop=(c == HC - 1))
            P.tt(x[:, m, :], x[:, m, :], bo[:, 0:TT], ALU.add)
        P.dma(xTv[:, :, t0:t0 + TT], x[:])
def gdn_core(cx, j):
    P = cx.P
    A = cx.A
    NS, L, T = cx.NS, cx.L, cx.T
    CH = 256
    NCH = L // CH
    hTv = cx.hT.rearrange("(k p) t -> p k t", p=128)
    yv = cx.yT.rearrange("(c p) t -> p c t", p=128)
    winv = A['gdn_w_in'][j].rearrange("(k p) n -> p k n", p=128)
    cw = P.sb([128, 32, 4], F32, name="g_cw")
    P.dma(cw[:], A['gdn_convw'][j])
    nw = P.sb([128, 2], F32, name="g_nw")
    P.dma(nw[:], A['gdn_normw'][j])
    dtb = P.sb([1, 8], F32, name="g_dtb")
    P.dma(dtb[:], A['gdn_dtb'][j])
    aneg = P.sb([1, 8], F32, name="g_aneg")
    P.dma(aneg[:], A['gdn_alog'][j])
    P.act(aneg[:], aneg[:], AF.Exp)
    P.ts(aneg[:], aneg[:], -1.0, ALU.mult)
    ident32 = cx.k32('ident'); ones32 = cx.k32('ones'); ule = cx.k32('ule'); ms = cx.k32('ms')
    negtri = cx.k32('negtri'); negtri_s = cx.k32('negtri_s')
    S1 = lambda n, shp, dt_=F32: P.sb(shp, dt_, name="g_" + n)
    wst = Rot([S1(f'wst{i}', [128, KC, 256]) for i in range(2)])
    wg = Rot([S1(f'wg{i}', [128, KC, 770], BF16) for i in range(2)])
    hbuf = Rot([S1(f'hb{i}', [128, KC, CH], BF16) for i in range(2)])
    pc = S1('pc', [128, 4, CH + 3]); sgate = S1('sgate', [128, 2, CH])
    bgT = S1('bgT', [1, 2, CH]); te = S1('te', [1, CH])
    cacc = Rot([S1(f'cacc{i}', [128, CH]) for i in range(2)])
    cvf = S1('cvf', [128, 4, CH]); sq = S1('sq', [128, 2, CH], BF16)
    rn = S1('rn', [128, CH]); tmp = S1('tmp', [128, CH])
    qkb = S1('qkb', [128, 2, CH], BF16)
    ktok = S1('ktok', [128, 128]); bgtok = S1('bgtok', [128, 2]); vb = S1('vb', [128, 256], BF16)
    gcol = S1('gcol', [128, 4])
    Gm = S1('Gm', [128, 128]); AO = S1('AO', [128, 128])
    decT = S1('decT', [128, 128], BF16); dec = S1('dec', [128, 128]); Erow = S1('Erow', [128, 128])
    Nk = Rot([S1(f'Nk{i}', [128, 128]) for i in range(2)])
    Ak = Rot([S1(f'Ak{i}', [128, 128]) for i in range(2)])
    Q = Rot([S1(f'Q{i}', [128, 128]) for i in range(2)])
    TTb = S1('TTb', [128, 128], BF16); Rk = S1('Rk', [128, 128], BF16); nwT = S1('nwT', [128, 128], BF16)
    vnew = S1('vnew', [128, 256], BF16); attT = S1('attT', [128, 128], BF16); qe = S1('qe', [128, 128], BF16)
    kdec = S1('kdec', [128, 128], BF16)
    oT = S1('oT', [128, 2, CH]); osq = S1('osq', [128, 2, CH], BF16); o2 = S1('o2', [128, 2, CH])
    ob = Rot([S1(f'ob{i}', [128, 2, CH], BF16) for i in range(2)])
    S32 = S1('S32', [128, 256]); Sb = S1('Sb', [128, 256], BF16)

    for h in range(8):
        w = wg.next()
        for (c0, n, d0) in [(128 * h, 128, 0), (1024 + 128 * h, 128, 128), (2048 + 256 * h, 256, 256),
                            (4096 + 256 * h, 256, 512), (6144 + h, 1, 768), (6152 + h, 1, 769)]:
            st = wst.next()
            P.dma(st[:, :, 0:n], winv[:, :, c0:c0 + n], **({'allow_slow_non_contiguous': True} if n == 1 else {}))
            P.copy(w[:, :, d0:d0 + n], st[:, :, 0:n], eng='pool')
        chs = [h, 8 + h, 16 + 2 * h, 17 + 2 * h]
        for s in range(NS):
            P.memset(S32[:], 0.0)
            P.memset(Sb[:], 0.0)
            P.memset(pc[:, :, 0:3], 0.0)
            for c in range(NCH):
                t0 = s * L + c * CH
                hb = hbuf.next()
                P.dma(hb[:], hTv[:, :, t0:t0 + CH])
                for i in range(4):
                    bk = cx.bank()
                    for k in range(KC):
                        P.mm(bk[:, 0:CH], w[:, k, 128 * i:128 * (i + 1)], hb[:, k, :], start=(k == 0), stop=(k == KC - 1))
                    P.copy(pc[:, i, 3:3 + CH], bk[:, 0:CH], eng=('act' if i % 2 else 'dve'))
                for i in range(2):
                    bk = cx.bank()
                    for k in range(KC):
                        P.mm(bk[:, 0:CH], w[:, k, 512 + 128 * i:512 + 128 * (i + 1)], hb[:, k, :], start=(k == 0), stop=(k == KC - 1))
                    P.act(sgate[:, i, :], bk[:, 0:CH], AF.Silu)
                bkb = cx.bank()
                for k in range(KC):
                    P.mm(bkb[0:1, 0:CH], w[:, k, 768:769], hb[:, k, :], start=(k == 0), stop=(k == KC - 1))
                P.act(bgT[:, 0, :], bkb[0:1, 0:CH], AF.Sigmoid)
                bka = cx.bank()
                for k in range(KC):
                    P.mm(bka[0:1, 0:CH], w[:, k, 769:770], hb[:, k, :], start=(k == 0), stop=(k == KC - 1))
                P.act(te[:], bka[0:1, 0:CH], AF.Exp, bias=dtb[0:1, h:h + 1])
                P.act(te[:], te[:], AF.Ln, bias=cx.oneb[0:1, :])
                P.ts(bgT[:, 1, :], te[:], aneg[0:1, h:h + 1], ALU.mult)
                for i in range(4):
                    acc = cacc.next()
                    ch = chs[i]
                    P.ts(acc[:], pc[:, i, 0:CH], cw[:, ch, 0:1], ALU.mult)
                    for k in range(1, 4):
                        P.stt(acc[:], pc[:, i, k:k + CH], cw[:, ch, k:k + 1], acc[:], ALU.mult, ALU.add)
                    P.act(cvf[:, i, :], acc[:], AF.Silu)
                P.copy(pc[:, :, 0:3], pc[:, :, CH:CH + 3])
                for i in range(2):
                    P.act(sq[:, i, :], cvf[:, i, :], AF.Square)
                    bk = cx.bank()
                    P.mm(bk[:, 0:CH], cx.k16('ones'), sq[:, i, :], start=True, stop=True)
                    rstd_from_sumsq(cx, rn[:], bk[:, 0:CH], 1, tmp[:])
                    if i == 0:
                        P.stt(cvf[:, 0, :], cvf[:, 0, :], 128 ** -0.5, rn[:], ALU.mult, ALU.mult)
                    else:
                        P.tt(cvf[:, 1, :], cvf[:, 1, :], rn[:], ALU.mult)
                P.copy(qkb[:], cvf[:, 0:2, :])
                for b in range(2):
                    tb = slice(128 * b, 128 * (b + 1))
                    bk = cx.bank()
                    P.transpose(bk[:, 0:128], cvf[:, 1, tb], ident32)
                    P.transpose(bk[:, 128:256], cvf[:, 2, tb], ident32)
                    P.transpose(bk[:, 256:384], cvf[:, 3, tb], ident32)
                    P.transpose(bk[:, 384:385], bgT[0:1, 0, tb], cx.k32('ident', 1, 1))
                    P.transpose(bk[:, 385:386], bgT[0:1, 1, tb], cx.k32('ident', 1, 1))
                    P.copy(bgtok[:], bk[:, 384:386])
                    P.copy(ktok[:], bk[:, 0:128], eng='act')
                    P.ts(vb[:], bk[:, 128:384], bgtok[:, 0:1], ALU.mult)
                    bc = cx.bank()
                    P.mm(bc[:, 0:1], ule, bgtok[:, 1:2], start=True, stop=True)
                    P.copy(gcol[:, 0:1], bc[:, 0:1])
                    P.act(gcol[:, 1:2], gcol[:, 0:1], AF.Exp)
                    P.tt(gcol[:, 2:3], gcol[:, 1:2], bgtok[:, 0:1], ALU.mult)
                    P.ts(gcol[:, 3:4], bgtok[:, 0:1], -1.0, ALU.mult)
                    P.ts(Gm[:], ms, bgtok[:, 1:2], ALU.mult)
                    P.ts(AO[:], ones32, bgtok[:, 1:2], ALU.mult, eng='pool')
                    bsT = cx.bank()
                    P.mm(bsT[:, 0:128], Gm[:], ule, start=True, stop=False)
                    P.mm(bsT[:, 0:128], ident32, negtri, start=False, stop=True)
                    P.act(decT[:], bsT[:, 0:128], AF.Exp)
                    bs_ = cx.bank()
                    P.mm(bs_[:, 0:128], ule, Gm[:], start=True, stop=False)
                    P.mm(bs_[:, 0:128], ident32, negtri_s, start=False, stop=True)
                    P.act(dec[:], bs_[:, 0:128], AF.Exp)
                    be = cx.bank()
                    P.mm(be[:, 0:128], AO[:], ule, start=True, stop=True)
                    P.act(Erow[:], be[:, 0:128], AF.Exp)
                    bkk = cx.bank()
                    P.mm(bkk[:, 0:128], qkb[:, 1, tb], qkb[:, 1, tb], start=True, stop=True)
                    n_ = Nk.next(); a_ = Ak.next(); q_ = Q.next()
                    P.stt(n_[:], bkk[:, 0:128], gcol[:, 3:4], dec[:], ALU.mult, ALU.mult)
                    bt = cx.bank()
                    P.transpose(bt[:, 0:128], n_[:], ident32)
                    P.copy(a_[:], bt[:, 0:128], eng='act')
                    P.tt(q_[:], bt[:, 0:128], ident32, ALU.add)
                    for lvl in range(6):
                        n2 = Nk.next(); a2 = Ak.next(); q2 = Q.next()
                        b1 = cx.bank()
                        P.mm(b1[:, 0:128], a_[:], n_[:], start=True, stop=True)
                        P.copy(n2[:], b1[:, 0:128], eng='act')
                        if lvl < 5:
                            b2 = cx.bank()
                            P.mm(b2[:, 0:128], n_[:], a_[:], start=True, stop=True)
                            P.copy(a2[:], b2[:, 0:128])
                        b3 = cx.bank()
                        P.mm(b3[:, 0:128], n2[:], q_[:], start=True, stop=True)
                        P.tt(q2[:], q_[:], b3[:, 0:128], ALU.add)
                        n_, a_, q_ = n2, a2, q2
                    P.copy(TTb[:], q_[:], eng='act')
                    P.ts(Rk[:], ktok[:], gcol[:, 2:3], ALU.mult)
                    bw = cx.bank()
                    P.mm(bw[:, 0:128], Rk[:], TTb[:], start=True, stop=True)
                    P.ts(nwT[:], bw[:, 0:128], -1.0, ALU.mult)
                    bv = cx.bank()
                    P.mm(bv[:, 0:256], TTb[:], vb[:], start=True, stop=False)
                    P.mm(bv[:, 0:256], nwT[:], Sb[:], start=False, stop=True)
                    P.copy(vnew[:], bv[:, 0:256], eng='act')
                    bq = cx.bank()
                    P.mm(bq[:, 0:128], qkb[:, 1, tb], qkb[:, 0, tb], start=True, stop=True)
                    P.tt(attT[:], bq[:, 0:128], decT[:], ALU.mult)
                    P.tt(qe[:], cvf[:, 0, tb], Erow[:], ALU.mult, eng='pool')
                    bo = cx.bank()
                    for e in range(2):
                        P.mm(bo[:, 128 * e:128 * (e + 1)], Sb[:, 128 * e:128 * (e + 1)], qe[:], start=True, stop=False)
                        P.mm(bo[:, 128 * e:128 * (e + 1)], vnew[:, 128 * e:128 * (e + 1)], attT[:], start=False, stop=True)
                    P.copy(oT[:, :, tb], bo[:, 0:256].rearrange("p (e t) -> p e t", e=2), eng='act')
                    P.ts(kdec[:], ktok[:], decT[:, 127:128], ALU.mult)
                    bS = cx.bank()
                    P.mm(bS[:, 0:256], kdec[:], vnew[:], start=True, stop=True)
                    P.stt(S32[:], S32[:], Erow[:, 127:128], bS[:, 0:256], ALU.mult, ALU.add)
                    P.copy(Sb[:], S32[:], eng='act')
                for e in range(2):
                    P.act(osq[:, e, :], oT[:, e, :], AF.Square)
                bn = cx.bank()
                for e in range(2):
                    P.mm(bn[:, 0:CH], cx.k16('ones'), osq[:, e, :], start=(e == 0), stop=(e == 1))
                rstd_from_sumsq(cx, rn[:], bn[:, 0:CH], 256, tmp[:])
                o_ = ob.next()
                for e in range(2):
                    P.stt(o2[:, e, :], oT[:, e, :], nw[:, e:e + 1], rn[:], ALU.mult, ALU.mult)
                    P.tt(o_[:, e, :], o2[:, e, :], sgate[:, e, :], ALU.mult, eng='pool')
                P.dma(yv[:, 2 * h:2 * h + 2, t0:t0 + CH], o_[:])


def phase_gdn(cx, xT, i_layer, j):
    nc = cx.P.nc
    phase_prenorm(cx, xT, cx.A['mix_norm_w'][i_layer], cx.hT)
    cx.P.finish()
    cx.P = Prog(nc)
    gdn_core(cx, j)
    cx.P.finish()
    cx.P = Prog(nc)
    phase_outproj(cx, xT, cx.yT, cx.A['gdn_w_out'][j], 2048)
def phase_prenorm(cx, xT, w_ap, hT):
    P = cx.P
    TT = 512
    T = cx.T
    w_sb = P.sb([128, KC], F32, name="pn_w")
    P.dma(w_sb[:], w_ap)
    xt = Rot([P.sb([128, KC, TT], F32, name=f"pn_x{i}") for i in range(2)])
    hs = Rot([P.sb([128, KC, TT], BF16, name=f"pn_h{i}") for i in range(2)])
    sq = P.sb([128, KC, TT], BF16, name="pn_sq")
    rstd = P.sb([128, TT], F32, name="pn_rstd")
    tmp = P.sb([128, TT], F32, name="pn_tmp")
    xTv = xT.rearrange("(k p) t -> p k t", p=128)
    hTv = hT.rearrange("(k p) t -> p k t", p=128)
    for ti in range(T // TT):
        t0 = ti * TT
        x = xt.next()
        h = hs.next()
        P.dma(x[:], xTv[:, :, t0:t0 + TT])
        rmsnorm_tile(cx, x, w_sb, h, TT, sq, rstd, tmp)
        P.dma(hTv[:, :, t0:t0 + TT], h[:])


def phase_outproj(cx, xT, yT, w_ap, K):
    P = cx.P
    TT = 512
    T = cx.T
    kc = K // 128
    w = P.sb([128, kc, D], BF16, name="op_w")
    load_w_bf16(cx, w, w_ap, K, D)
    xt = Rot([P.sb([128, KC, TT], F32, name=f"op_x{i}") for i in range(2)])
    ys = Rot([P.sb([128, kc, TT], BF16, name=f"op_y{i}") for i in range(2)])
    xTv = xT.rearrange("(k p) t -> p k t", p=128)
    yTv = yT.rearrange("(k p) t -> p k t", p=128)
    for ti in range(T // TT):
        t0 = ti * TT
        x = xt.next()
        y = ys.next()
        P.dma(x[:], xTv[:, :, t0:t0 + TT])
        P.dma(y[:], yTv[:, :, t0:t0 + TT])
        for m in range(KC):
            bo = cx.bank()
            for c in range(kc):
                P.mm(bo[:, 0:TT], w[:, c, m * 128:(m + 1) * 128], y[:, c, :], start=(c == 0), stop=(c == kc - 1))
            P.tt(x[:, m, :], x[:, m, :], bo[:, 0:TT], ALU.add)
        P.dma(xTv[:, :, t0:t0 + TT], x[:])


def mamba_core(cx, j):
    P = cx.P
    A = cx.A
    NS, L = cx.NS, cx.L
    CH = 256
    NCH = L // CH
    hTv = cx.hT.rearrange("(k p) t -> p k t", p=128)
    y3v = cx.yT.rearrange("(c p) t -> p c t", p=128)
    winv = A['mamba_w_in'][j].rearrange("(k p) n -> p k n", p=128)
    cw = P.sb([128, 32, 4], F32, name="m_cw")
    P.dma(cw[:], A['mamba_convw'][j])
    cb = P.sb([128, 32], F32, name="m_cb")
    P.dma(cb[:], A['mamba_convb'][j])
    dcol = P.sb([128, 16], F32, name="m_dcol")
    P.dma(dcol[:], A['mamba_drep'][j])
    nw = P.sb([128, 16], F32, name="m_nw")
    P.dma(nw[:], A['mamba_normw'][j])
    dtb = P.sb([4, 8], F32, name="m_dtb")
    P.dma(dtb[:], A['mamba_dtb'][j])
    aneg = P.sb([4, 8], F32, name="m_aneg")
    P.dma(aneg[:], A['mamba_alog'][j])
    P.act(aneg[:], aneg[:], AF.Exp)
    P.ts(aneg[:], aneg[:], -1.0, ALU.mult)
    ones32 = cx.k32('ones')
    ident32 = cx.k32('ident')
    u0 = cx.c32[:, cx.offs['ule']:cx.offs['ule'] + 256]
    ule = cx.k32('ule')
    negtri = cx.k32('negtri')
    ms = cx.k32('ms')

    wg = Rot([P.sb([128, KC, 772], BF16, name=f"m_wg{i}") for i in range(2)])
    hbuf = Rot([P.sb([128, KC, CH], BF16, name=f"m_hb{i}") for i in range(2)])
    sz = P.sb([128, 2, CH], F32, name="m_sz")
    pc = P.sb([128, 4, CH + 3], F32, name="m_pc")
    da = P.sb([4, 2, CH], F32, name="m_da")
    dte = P.sb([4, CH], F32, name="m_dte")
    cacc = Rot([P.sb([128, CH], F32, name=f"m_cacc{i}") for i in range(2)])
    cvf = P.sb([128, 4, CH], F32, name="m_cvf")
    bct = P.sb([128, 2, CH], BF16, name="m_bct")
    datok = P.sb([128, 2, 8], F32, name="m_datok")
    xtb = P.sb([128, 2, 256], BF16, name="m_xtb")
    btok = P.sb([128, 2, 128], BF16, name="m_btok")
    AM = P.sb([128, 2, 4, 128], F32, name="m_AM")
    AO = P.sb([128, 2, 4, 128], F32, name="m_AO")
    dec = P.sb([128, 4, 384], BF16, name="m_dec")
    E = P.sb([128, 4, CH], F32, name="m_E")
    gT = P.sb([128, 384], F32, name="m_gT")
    Mr = Rot([P.sb([128, 384], BF16, name=f"m_Mr{i}") for i in range(2)])
    Ctl = Rot([P.sb([128, CH], BF16, name=f"m_Ct{i}") for i in range(2)])
    ysb = P.sb([128, 2, CH], F32, name="m_ysb")
    y2 = P.sb([128, 2, CH], F32, name="m_y2")
    ysq = P.sb([128, 2, CH], BF16, name="m_ysq")
    rstd = P.sb([128, CH], F32, name="m_rstd")
    tmp = P.sb([128, CH], F32, name="m_tmp")
    y3 = Rot([P.sb([128, 2, CH], BF16, name=f"m_y3{i}") for i in range(2)])
    xw = P.sb([128, 2, 256], BF16, name="m_xw")
    st32 = P.sb([128, 256], F32, name="m_st32")
    stb = P.sb([128, 256], BF16, name="m_stb")

    wst = Rot([P.sb([128, KC, 256], F32, name=f"m_wst{i}") for i in range(2)])
    for g in range(8):
        w = wg.next()
        for (c0, n, d0) in [(256 * g, 256, 0), (2048 + 256 * g, 256, 256), (4096 + 128 * g, 128, 512),
                            (5120 + 128 * g, 128, 640), (6144 + 4 * g, 4, 768)]:
            st = wst.next()
            P.dma(st[:, :, 0:n], winv[:, :, c0:c0 + n])
            P.copy(w[:, :, d0:d0 + n], st[:, :, 0:n], eng='pool')
        chs = [2 * g, 2 * g + 1, 16 + g, 24 + g]
        for s in range(NS):
            P.memset(st32[:], 0.0)
            P.memset(stb[:], 0.0)
            P.memset(pc[:, :, 0:3], 0.0)
            for c in range(NCH):
                t0 = s * L + c * CH
                hb = hbuf.next()
                P.dma(hb[:], hTv[:, :, t0:t0 + CH])
                for i in range(2):
                    bk = cx.bank()
                    for k in range(KC):
                        P.mm(bk[:, 0:CH], w[:, k, 128 * i:128 * (i + 1)], hb[:, k, :], start=(k == 0), stop=(k == KC - 1))
                    P.act(sz[:, i, :], bk[:, 0:CH], AF.Silu)
                for i in range(4):
                    bk = cx.bank()
                    for k in range(KC):
                        P.mm(bk[:, 0:CH], w[:, k, 256 + 128 * i:256 + 128 * (i + 1)], hb[:, k, :], start=(k == 0), stop=(k == KC - 1))
                    P.copy(pc[:, i, 3:3 + CH], bk[:, 0:CH], eng=('act' if i % 2 else 'dve'))
                bk = cx.bank()
                for k in range(KC):
                    P.mm(bk[0:4, 0:CH], w[:, k, 768:772], hb[:, k, :], start=(k == 0), stop=(k == KC - 1))
                P.act(dte[:], bk[0:4, 0:CH], AF.Exp, bias=dtb[:, g:g + 1])
                P.act(da[:, 0, :], dte[:], AF.Ln, bias=cx.oneb[0:4, :])
                P.ts(da[:, 1, :], da[:, 0, :], aneg[:, g:g + 1], ALU.mult)
                for i in range(4):
                    acc = cacc.next()
                    ch = chs[i]
                    P.ts(acc[:], pc[:, i, 0:CH], cw[:, ch, 0:1], ALU.mult)
                    for k in range(1, 4):
                        P.stt(acc[:], pc[:, i, k:k + CH], cw[:, ch, k:k + 1], acc[:], ALU.mult, ALU.add)
                    P.act(cvf[:, i, :], acc[:], AF.Silu, bias=cb[:, ch:ch + 1])
                P.copy(pc[:, :, 0:3], pc[:, :, CH:CH + 3])
                P.copy(bct[:, :, :], cvf[:, 2:4, :])
                for b in range(2):
                    bk = cx.bank()
                    for i in range(3):
                        P.transpose(bk[:, 128 * i:128 * (i + 1)], cvf[:, i, 128 * b:128 * (b + 1)], ident32)
                    P.transpose(bk[:, 384:388], da[0:4, 0, 128 * b:128 * (b + 1)], cx.k32('ident', 4, 4))
                    P.transpose(bk[:, 388:392], da[0:4, 1, 128 * b:128 * (b + 1)], cx.k32('ident', 4, 4))
                    P.copy(datok[:, b, :], bk[:, 384:392])
                    P.copy(btok[:, b, :], bk[:, 256:384], eng='act')
                    dtbc = datok[:, b, 0:4].unsqueeze(2).to_broadcast([128, 4, 64])
                    P.tt(xtb[:, b, :].rearrange("p (r q) -> p r q", r=4), bk[:, 0:256].rearrange("p (r q) -> p r q", r=4), dtbc, ALU.mult)
                    for r in range(4):
                        P.ts(AM[:, b, r, :], ms, datok[:, b, 4 + r:5 + r], ALU.mult)
                        P.ts(AO[:, b, r, :], ones32, datok[:, b, 4 + r:5 + r], ALU.mult, eng='pool')
                bG = cx.bank()
                P.mm(bG[:, 0:256], bct[:, 0, 0:128], bct[:, 1, 0:256], start=True, stop=True)
                P.mm(bG[:, 256:384], bct[:, 0, 128:256], bct[:, 1, 128:256], start=True, stop=True)
                P.copy(gT[:], bG[:, 0:384], eng='act')
                for r in range(4):
                    bs_ = cx.bank()
                    P.mm(bs_[:, 0:256], AM[:, 0, r, :], u0, start=True, stop=False)
                    P.mm(bs_[:, 128:256], AO[:, 1, r, :], ule, start=False, stop=False)
                    P.mm(bs_[:, 0:128], ident32, negtri, start=False, stop=True)
                    P.mm(bs_[:, 256:384], AM[:, 1, r, :], ule, start=True, stop=False)
                    P.mm(bs_[:, 256:384], ident32, negtri, start=False, stop=True)
                    P.act(dec[:, r, :], bs_[:, 0:384], AF.Exp)
                    ba = cx.bank()
                    P.mm(ba[:, 0:256], AO[:, 0, r, :], u0, start=True, stop=False)
                    P.mm(ba[:, 128:256], AO[:, 1, r, :], ule, start=False, stop=True)
                    P.act(E[:, r, :], ba[:, 0:256], AF.Exp)
                for i in range(2):
                    by = cx.bank()
                    for rr in range(2):
                        r = 2 * i + rr
                        m_ = Mr.next()
                        ct = Ctl.next()
                        P.tt(m_[:], gT[:], dec[:, r, :], ALU.mult)
                        P.tt(ct[:], cvf[:, 3, :], E[:, r, :], ALU.mult, eng='pool')
                        o = by[64 * rr:64 * rr + 64, 0:256]
                        P.mm(o, xtb[:, 0, 64 * r:64 * r + 64], m_[:, 0:256], start=True, stop=False)
                        P.mm(by[64 * rr:64 * rr + 64, 128:256], xtb[:, 1, 64 * r:64 * r + 64], m_[:, 256:384], start=False, stop=False)
                        P.mm(o, stb[:, 64 * r:64 * r + 64], ct[:], start=False, stop=True)
                    P.stt(ysb[:, i, :], cvf[:, i, :], dcol[:, 2 * g + i:2 * g + i + 1], by[:, 0:256], ALU.mult, ALU.add)
                    P.tt(y2[:, i, :], ysb[:, i, :], sz[:, i, :], ALU.mult, eng='pool')
                    P.act(ysq[:, i, :], y2[:, i, :], AF.Square)
                bq = cx.bank()
                for i in range(2):
                    P.mm(bq[:, 0:CH], cx.k16('ones'), ysq[:, i, :], start=(i == 0), stop=(i == 1))
                rstd_from_sumsq(cx, rstd[:], bq[:, 0:CH], 256, tmp[:])
                yo = y3.next()
                for i in range(2):
                    P.stt(yo[:, i, :], y2[:, i, :], nw[:, 2 * g + i:2 * g + i + 1], rstd[:], ALU.mult, ALU.mult)
                P.dma(y3v[:, 2 * g:2 * g + 2, t0:t0 + CH], yo[:])
                for b in range(2):
                    for r in range(4):
                        col = 255 if b == 0 else 383
                        P.ts(xw[:, b, 64 * r:64 * r + 64], xtb[:, b, 64 * r:64 * r + 64], dec[:, r, col:col + 1], ALU.mult)
                bst = cx.bank()
                P.mm(bst[:, 0:256], btok[:, 0, :], xw[:, 0, :], start=True, stop=False)
                P.mm(bst[:, 0:256], btok[:, 1, :], xw[:, 1, :], start=False, stop=True)
                for r in range(4):
                    P.stt(st32[:, 64 * r:64 * r + 64], st32[:, 64 * r:64 * r + 64], E[:, r, 255:256], bst[:, 64 * r:64 * r + 64], ALU.mult, ALU.add)
                P.copy(stb[:], st32[:])


def phase_mamba(cx, xT, i_layer, j):
    nc = cx.P.nc
    phase_prenorm(cx, xT, cx.A['mix_norm_w'][i_layer], cx.hT)
    cx.P.finish()
    cx.P = Prog(nc)
    mamba_core(cx, j)
    cx.P.finish()
    cx.P = Prog(nc)
    phase_outproj(cx, xT, cx.yT, cx.A['mamba_w_out'][j], 2048)
import math as _math


def s5_prep(cx, j):
    P = cx.P
    A = cx.A
    NS, L, T = cx.NS, cx.L, cx.T
    NP = 32
    H = L // 2
    npass = int(round(_math.log2(L)))
    PI = _math.pi
    hTv = cx.hT.rearrange("(k p) t -> p k t", p=128)
    gTv = cx.yT.rearrange("(k p) t -> p k t", p=128)
    f = lambda n, shp: P.sb(shp, F32, name="s5_" + n)
    lr = f('lr', [128, NP]); li = f('li', [128, NP]); dt = f('dt', [128, NP])
    P.dma(lr[:], A['s5_lr'][j]); P.dma(li[:], A['s5_li'][j]); P.dma(dt[:], A['s5_ldt'][j])
    bre = f('bre', [128, NP, 16]); bim = f('bim', [128, NP, 16]); cre = f('cre', [128, NP, 16]); cim = f('cim', [128, NP, 16])
    P.dma(bre[:], A['s5_bre'][j]); P.dma(bim[:], A['s5_bim'][j]); P.dma(cre[:], A['s5_cre'][j]); P.dma(cim[:], A['s5_cim'][j])
    dcol = f('dcol', [128, KC]); P.dma(dcol[:], A['s5_dk'][j])
    P.act(dt[:], dt[:], AF.Exp)
    mag = f('mag', [128, NP]); th = f('th', [128, NP]); t1 = f('t1', [128, NP]); t2 = f('t2', [128, NP])
    P.tt(mag[:], lr[:], dt[:], ALU.mult)
    P.act(mag[:], mag[:], AF.Exp)
    P.tt(th[:], li[:], dt[:], ALU.mult)

    def sin_of(out, ang, shift):
        q = f('q%d' % P.nbuf, [128, NP]); qi = P.sb([128, NP], mybir.dt.int32, name='s5_qi%d' % P.nbuf); r = f('r%d' % P.nbuf, [128, NP]); m = f('m%d' % P.nbuf, [128, NP])
        P.ts(q[:], ang[:], shift, ALU.add, 1.0 / (2 * PI), ALU.mult)
        P.copy(qi[:], q[:])
        P.copy(q[:], qi[:])
        P.ts(r[:], ang[:], shift, ALU.add)
        P.stt(r[:], q[:], -2 * PI, r[:], ALU.mult, ALU.add)
        P.ts(m[:], r[:], PI, ALU.is_gt)
        P.stt(r[:], m[:], -2 * PI, r[:], ALU.mult, ALU.add)
        P.ts(m[:], r[:], -PI, ALU.is_lt)
        P.stt(r[:], m[:], 2 * PI, r[:], ALU.mult, ALU.add)
        P.ts(r[:], r[:], PI, ALU.min, -PI, ALU.max)
        P.act(out[:], r[:], AF.Sin)

    sn = f('sn', [128, NP]); cs = f('cs', [128, NP])
    sin_of(sn, th, 0.0)
    sin_of(cs, th, PI / 2)
    pw = f('pw', [128, npass, 3, NP])
    P.tt(pw[:, 0, 0, :], mag[:], cs[:], ALU.mult)
    P.tt(pw[:, 0, 1, :], mag[:], sn[:], ALU.mult)
    for k in range(1, npass):
        P.tt(t1[:], pw[:, k - 1, 0, :], pw[:, k - 1, 0, :], ALU.mult)
        P.tt(t2[:], pw[:, k - 1, 1, :], pw[:, k - 1, 1, :], ALU.mult)
        P.tt(pw[:, k, 0, :], t1[:], t2[:], ALU.subtract)
        P.tt(t1[:], pw[:, k - 1, 0, :], pw[:, k - 1, 1, :], ALU.mult)
        P.ts(pw[:, k, 1, :], t1[:], 2.0, ALU.mult)
    for k in range(npass):
        P.ts(pw[:, k, 2, :], pw[:, k, 1, :], -1.0, ALU.mult)
    den = f('den', [128, NP]); zr = f('zr', [128, NP]); zi = f('zi', [128, NP]); lm1 = f('lm1', [128, NP])
    P.tt(t1[:], lr[:], lr[:], ALU.mult)
    P.tt(t2[:], li[:], li[:], ALU.mult)
    P.tt(den[:], t1[:], t2[:], ALU.add)
    P.recip(den[:], den[:])
    P.ts(lm1[:], pw[:, 0, 0, :], -1.0, ALU.add)
    P.tt(t1[:], lm1[:], lr[:], ALU.mult)
    P.tt(t2[:], pw[:, 0, 1, :], li[:], ALU.mult)
    P.tt(zr[:], t1[:], t2[:], ALU.add)
    P.tt(zr[:], zr[:], den[:], ALU.mult)
    P.tt(t1[:], pw[:, 0, 1, :], lr[:], ALU.mult)
    P.tt(t2[:], lm1[:], li[:], ALU.mult)
    P.tt(zi[:], t1[:], t2[:], ALU.subtract)
    P.tt(zi[:], zi[:], den[:], ALU.mult)
    bbr = f('bbr', [128, NP, 16]); bbi = f('bbi', [128, NP, 16]); t3 = f('t3', [128, NP, 16])
    zrb = zr[:, :].unsqueeze(2).to_broadcast([128, NP, 16])
    zib = zi[:, :].unsqueeze(2).to_broadcast([128, NP, 16])
    P.tt(bbr[:], bre[:], zrb, ALU.mult)
    P.tt(t3[:], bim[:], zib, ALU.mult)
    P.tt(bbr[:], bbr[:], t3[:], ALU.subtract)
    P.tt(bbi[:], bim[:], zrb, ALU.mult)
    P.tt(t3[:], bre[:], zib, ALU.mult)
    P.tt(bbi[:], bbi[:], t3[:], ALU.add)
    P.ts(cim[:], cim[:], -1.0, ALU.mult)
    WB = P.sb([128, NP, 2, 128], BF16, name="s5_WB")
    WC = P.sb([128, NP, 2, 128], BF16, name="s5_WC")
    P.memset(WC[:], 0.0)
    zb = Rot([f('zb%d' % i, [128, 128]) for i in range(2)])
    for p in range(NP):
        pp = p % 4
        for ri, (bsrc, csrc) in enumerate([(bbr, cre), (bbi, cim)]):
            z = zb.next()
            P.memset(z[:], 0.0, eng='pool')
            P.copy(z[0:64, 32 * pp:32 * pp + 16], bsrc[0:64, p, :], eng='pool')
            P.copy(z[64:128, 32 * pp + 16:32 * pp + 32], bsrc[64:128, p, :], eng='pool')
            bk = cx.bank()
            P.transpose(bk[:, 0:128], z[:], cx.k32('ident'))
            P.copy(WB[:, p, ri, :], bk[:, 0:128], eng='act')
            P.copy(WC[0:64, p, ri, 32 * pp:32 * pp + 16], csrc[0:64, p, :])
            P.copy(WC[64:128, p, ri, 32 * pp + 16:32 * pp + 32], csrc[64:128, p, :])
    P.dma(cx.s5_WB, WB[:].rearrange("p a b c -> p (a b c)"))
    P.dma(cx.s5_WC, WC[:].rearrange("p a b c -> p (a b c)"))
    P.dma(cx.s5_pw[:, 0:npass * 3 * NP], pw[:].rearrange("p a b c -> p (a b c)"))
    P.dma(cx.s5_dcol, dcol[:])


def s5_core(cx, j):
    P = cx.P
    A = cx.A
    NS, L, T = cx.NS, cx.L, cx.T
    NP = 32
    H = L // 2
    npass = int(round(_math.log2(L)))
    hTv = cx.hT.rearrange("(k p) t -> p k t", p=128)
    gTv = cx.yT.rearrange("(k p) t -> p k t", p=128)
    WB = P.sb([128, NP, 2, 128], BF16, name="s5_WB")
    WC = P.sb([128, NP, 2, 128], BF16, name="s5_WC")
    pw = P.sb([128, npass, 3, NP], F32, name="s5_pw")
    dcol = P.sb([128, KC], F32, name="s5_dcol")
    P.dma(WB[:].rearrange("p a b c -> p (a b c)"), cx.s5_WB)
    P.dma(WC[:].rearrange("p a b c -> p (a b c)"), cx.s5_WC)
    P.dma(pw[:].rearrange("p a b c -> p (a b c)"), cx.s5_pw[:, 0:npass * 3 * NP])
    P.dma(dcol[:], cx.s5_dcol)
    W = H + L
    bufs = [[P.sb([128, W], F32, name=f"s5_s{a}{b}") for b in range(2)] for a in range(2)]
    for a in range(2):
        for b in range(2):
            P.memset(bufs[a][b][:, 0:H], 0.0)
    CT = min(L, 1024)
    tAr = Rot([P.sb([128, CT], F32, name=f"s5_tA{i}") for i in range(2)])
    tBr = Rot([P.sb([128, CT], F32, name=f"s5_tB{i}") for i in range(2)])
    fin = [P.sb([128, L], BF16, name=f"s5_fin{b}") for b in range(2)]
    yacc = P.sb([128, L], F32, name="s5_yacc")
    u = P.sb([128, L], BF16, name="s5_u")
    gel = Rot([P.sb([128, 512], F32, name=f"s5_g{i}") for i in range(4)])
    gout = Rot([P.sb([128, 512], BF16, name=f"s5_go{i}") for i in range(2)])
    NT = L // 512
    for kc in range(KC):
        for s in range(NS):
            P.dma(u[:], hTv[:, kc, s * L:(s + 1) * L])
            for pp in range(4):
                p = kc * 4 + pp
                cur = 0
                for ti in range(NT):
                    for ri in range(2):
                        bk = cx.bank()
                        P.mm(bk[:, 0:512], WB[:, p, ri, :], u[:, ti * 512:(ti + 1) * 512], start=True, stop=True)
                        P.copy(bufs[0][ri][:, H + ti * 512:H + (ti + 1) * 512], bk[:, 0:512], eng='act')
                for k in range(npass):
                    d = 1 << k
                    src = bufs[cur]
                    last = (k == npass - 1)
                    dst = fin if last else bufs[1 - cur]
                    pr = pw[:, k, 0, p:p + 1]
                    pi_ = pw[:, k, 1, p:p + 1]
                    npi = pw[:, k, 2, p:p + 1]
                    for c0 in range(0, L, CT):
                        o0 = c0 if last else H + c0
                        o_re = dst[0][:, o0:o0 + CT]
                        o_im = dst[1][:, o0:o0 + CT]
                        tA = tAr.next(); tB = tBr.next()
                        P.stt(tA[:], src[0][:, H + c0 - d:H + c0 - d + CT], pr, src[0][:, H + c0:H + c0 + CT], ALU.mult, ALU.add)
                        P.stt(o_re, src[1][:, H + c0 - d:H + c0 - d + CT], npi, tA[:], ALU.mult, ALU.add)
                        P.stt(tB[:], src[1][:, H + c0 - d:H + c0 - d + CT], pr, src[1][:, H + c0:H + c0 + CT], ALU.mult, ALU.add)
                        P.stt(o_im, src[0][:, H + c0 - d:H + c0 - d + CT], pi_, tB[:], ALU.mult, ALU.add)
                    cur = 1 - cur
                for ti in range(NT):
                    bk = cx.bank()
                    P.mm(bk[:, 0:512], WC[:, p, 0, :], fin[0][:, ti * 512:(ti + 1) * 512], start=True, stop=False)
                    P.mm(bk[:, 0:512], WC[:, p, 1, :], fin[1][:, ti * 512:(ti + 1) * 512], start=False, stop=True)
                    if pp == 0:
                        P.copy(yacc[:, ti * 512:(ti + 1) * 512], bk[:, 0:512], eng='act')
                    else:
                        P.tt(yacc[:, ti * 512:(ti + 1) * 512], yacc[:, ti * 512:(ti + 1) * 512], bk[:, 0:512], ALU.add, eng='dve')
            for ti in range(NT):
                sl = slice(ti * 512, (ti + 1) * 512)
                y = gel.next(); y2 = gel.next(); z = gel.next()
                P.stt(y[:], u[:, ti * 512:(ti + 1) * 512], dcol[:, kc:kc + 1], yacc[:, sl], ALU.mult, ALU.add)
                P.act(y2[:], y[:], AF.Square)
                P.ts(y2[:], y2[:], 0.044715, ALU.mult, 1.0, ALU.add)
                P.tt(z[:], y2[:], y[:], ALU.mult, eng='pool')
                P.act(z[:], z[:], AF.Sigmoid, scale=1.5957691216057308)
                go = gout.next()
                P.tt(go[:], y[:], z[:], ALU.mult, eng='pool')
                P.dma(gTv[:, kc, s * L + ti * 512:s * L + (ti + 1) * 512], go[:])


def s5_glu(cx, xT, j):
    P = cx.P
    A = cx.A
    T = cx.T
    TT = 512
    w = P.sb([128, KC, 2 * D], BF16, name="glu_w")
    load_w_bf16(cx, w, A['s5_w_glu'][j], D, 2 * D)
    bg = P.sb([128, 16], F32, name="glu_b")
    P.dma(bg[:], A['s5_bglu'][j])
    xt = Rot([P.sb([128, KC, TT], F32, name=f"glu_x{i}") for i in range(2)])
    gs = Rot([P.sb([128, KC, TT], BF16, name=f"glu_g{i}") for i in range(2)])
    sg = Rot([P.sb([128, TT], F32, name=f"glu_s{i}") for i in range(2)])
    oo = Rot([P.sb([128, TT], F32, name=f"glu_o{i}") for i in range(2)])
    xTv = xT.rearrange("(k p) t -> p k t", p=128)
    gTv = cx.yT.rearrange("(k p) t -> p k t", p=128)
    for ti in range(T // TT):
        t0 = ti * TT
        x = xt.next(); g = gs.next()
        P.dma(x[:], xTv[:, :, t0:t0 + TT])
        P.dma(g[:], gTv[:, 0:KC, t0:t0 + TT])
        for m in range(KC):
            ba = cx.bank(); bb = cx.bank()
            for k in range(KC):
                P.mm(ba[:, 0:TT], w[:, k, m * 128:(m + 1) * 128], g[:, k, :], start=(k == 0), stop=(k == KC - 1))
            for k in range(KC):
                P.mm(bb[:, 0:TT], w[:, k, D + m * 128:D + (m + 1) * 128], g[:, k, :], start=(k == 0), stop=(k == KC - 1))
            s_ = sg.next(); o = oo.next()
            P.act(s_[:], bb[:, 0:TT], AF.Sigmoid, bias=bg[:, 8 + m:9 + m])
            P.stt(o[:], ba[:, 0:TT], bg[:, m:m + 1], s_[:], ALU.add, ALU.mult)
            P.tt(x[:, m, :], x[:, m, :], o[:], ALU.add, eng='pool')
        P.dma(xTv[:, :, t0:t0 + TT], x[:])


def phase_s5(cx, xT, i_layer, j):
    nc = cx.P.nc
    phase_prenorm(cx, xT, cx.A['mix_norm_w'][i_layer], cx.hT)
    cx.P.finish()
    cx.P = Prog(nc)
    s5_prep(cx, j)
    cx.P.finish()
    cx.P = Prog(nc)
    s5_core(cx, j)
    cx.P.finish()
    cx.P = Prog(nc)
    s5_glu(cx, xT, j)
DEPTH = 4
W_SHAPES = {
    'mix_norm_w': (4, 128, 8), 'ffn_norm_w': (4, 128, 8), 'ffn_w_in': (4, 1024, 5632), 'ffn_w_out': (4, 2816, 1024),
    'final_norm_w': (128, 8),
}


def build(NS, L, plan, extra_shapes=None):
    nc = bass.Bass("TRN2", target_bir_lowering=False)
    P = Prog(nc)
    consts, offs = make_consts()
    T = NS * L
    A = {}
    shapes = dict(W_SHAPES)
    if extra_shapes:
        shapes.update(extra_shapes)
    for name, shp in shapes.items():
        A[name] = P.dram(name, shp, F32, kind="ExternalInput")
    x_in = P.dram("x", [NS, L, D], F32, kind="ExternalInput")
    c_in = P.dram("consts", list(consts.shape), F32, kind="ExternalInput")
    out = P.dram("out", [NS, L, D], F32, kind="ExternalOutput")
    xT = P.dram("xT", [D, T], F32)
    cx = Ctx(P, c_in, offs)
    cx.hT = P.dram("hT", [D, T], BF16)
    cx.yT = P.dram("yT", [2048, T], BF16)
    cx.s5_WB = P.dram("s5WB", [128, 32 * 2 * 128], BF16)
    cx.s5_WC = P.dram("s5WC", [128, 32 * 2 * 128], BF16)
    cx.s5_pw = P.dram("s5pw", [128, 16 * 3 * 32], F32)
    cx.s5_dcol = P.dram("s5dcol", [128, 8], F32)
    cx.A = A
    cx.NS, cx.L, cx.T = NS, L, T
    for ph in plan:
        if cx.P is None:
            cx.P = Prog(nc)
        if ph == 'load':
            phase_load(cx, x_in, xT, NS, L)
        elif ph == 'final':
            phase_final(cx, xT, A['final_norm_w'], out, NS, L)
        elif ph[0] == 'ffn':
            i = ph[1]
            phase_ffn(cx, xT, A['ffn_norm_w'][i], A['ffn_w_in'][i], A['ffn_w_out'][i], T)
        elif ph[0] == 'mamba':
            phase_mamba(cx, xT, ph[1], ph[2])
        elif ph[0] == 's5':
            phase_s5(cx, xT, ph[1], ph[2])
        elif ph[0] == 'gdn':
            phase_gdn(cx, xT, ph[1], ph[2])
        cx.P.finish()
        cx.P = None
    return nc, consts
def _pk(v, n):
    return np.ascontiguousarray(np.asarray(v, np.float32).reshape(n, 128).T)


def prep_weights(inp):
    o = {}
    f = lambda a: np.ascontiguousarray(np.asarray(a, np.float32))
    o['mix_norm_w'] = np.stack([_pk(v, 8) for v in inp['mix_norm_w']])
    o['ffn_norm_w'] = np.stack([_pk(v, 8) for v in inp['ffn_norm_w']])
    o['final_norm_w'] = _pk(inp['final_norm_w'], 8)
    o['ffn_w_in'] = f(inp['ffn_w_in'])
    o['ffn_w_out'] = f(inp['ffn_w_out'])
    if 'mamba_w_in' in inp:
        o['mamba_w_in'] = f(inp['mamba_w_in'])
        o['mamba_w_out'] = f(inp['mamba_w_out'])
        cw = np.asarray(inp['mamba_conv_w'], np.float32)
        o['mamba_convw'] = np.ascontiguousarray(cw.transpose(0, 2, 1).reshape(-1, 32, 128, 4).transpose(0, 2, 1, 3))
        o['mamba_convb'] = np.stack([_pk(v, 32) for v in inp['mamba_conv_b']])
        o['mamba_drep'] = np.stack([_pk(np.repeat(np.asarray(v, np.float32), 64), 16) for v in inp['mamba_d']])
        o['mamba_normw'] = np.stack([_pk(v, 16) for v in inp['mamba_norm_w']])
        o['mamba_dtb'] = np.stack([np.ascontiguousarray(np.asarray(v, np.float32).reshape(8, 4).T) for v in inp['mamba_dt_bias']])
        o['mamba_alog'] = np.stack([np.ascontiguousarray(np.asarray(v, np.float32).reshape(8, 4).T) for v in inp['mamba_a_log']])
    if 's5_lam_re' in inp:
        prep_s5(inp, o)
    if 'gdn_w_in' in inp:
        prep_gdn(inp, o)
    return o


def prep_s5(inp, o):
    f = lambda a: np.asarray(a, np.float32)
    n = len(inp['s5_lam_re'])
    o['s5_lr'] = np.stack([_pk(f(v).ravel(), 32) for v in inp['s5_lam_re']])
    o['s5_li'] = np.stack([_pk(f(v).ravel(), 32) for v in inp['s5_lam_im']])
    o['s5_ldt'] = np.stack([_pk(np.repeat(f(v), 64), 32) for v in inp['s5_log_dt']])
    for nm, src in [('s5_bre', 's5_b_re'), ('s5_bim', 's5_b_im')]:
        o[nm] = np.stack([np.ascontiguousarray(f(v).reshape(32, 128, 16).transpose(1, 0, 2)) for v in inp[src]])
    for nm, src in [('s5_cre', 's5_c_re'), ('s5_cim', 's5_c_im')]:
        o[nm] = np.stack([np.ascontiguousarray(f(v).reshape(32, 2, 16, 64).transpose(1, 3, 0, 2).reshape(128, 32, 16)) for v in inp[src]])
    o['s5_dk'] = np.stack([_pk(v, 8) for v in inp['s5_d']])
    o['s5_w_glu'] = np.ascontiguousarray(f(inp['s5_w_glu']))
    o['s5_bglu'] = np.stack([_pk(v, 16) for v in inp['s5_b_glu']])
    return o


def prep_gdn(inp, o):
    f = lambda a: np.asarray(a, np.float32)
    o['gdn_w_in'] = np.ascontiguousarray(f(inp['gdn_w_in']))
    o['gdn_w_out'] = np.ascontiguousarray(f(inp['gdn_w_out']))
    cw = f(inp['gdn_conv_w'])
    o['gdn_convw'] = np.ascontiguousarray(cw.transpose(0, 2, 1).reshape(-1, 32, 128, 4).transpose(0, 2, 1, 3))
    o['gdn_normw'] = np.stack([_pk(v, 2) for v in inp['gdn_norm_w']])
    o['gdn_dtb'] = np.ascontiguousarray(f(inp['gdn_dt_bias']).reshape(-1, 1, 8))
    o['gdn_alog'] = np.ascontiguousarray(f(inp['gdn_a_log']).reshape(-1, 1, 8))
    return o


PLAN = ['load', ('mamba', 0, 0), ('ffn', 0), ('s5', 1, 0), ('ffn', 1), ('gdn', 2, 0), ('ffn', 2),
        ('mamba', 3, 1), ('ffn', 3), 'final']


def kernel(**inputs):
    x = np.ascontiguousarray(np.asarray(inputs['x'], np.float32))
    W = prep_weights(inputs)
    B, L, _ = x.shape
    n_cores = 8
    NS = B // n_cores
    shapes = {k: tuple(v.shape) for k, v in W.items()}
    nc, consts = build(NS, L, PLAN, extra_shapes=shapes)
    in_maps = []
    for c in range(n_cores):
        m = dict(W)
        m['x'] = x[NS * c:NS * (c + 1)]
        m['consts'] = consts
        in_maps.append(m)
    res = run_bass_kernel_spmd(nc, in_maps, core_ids=list(range(n_cores)))
    return np.concatenate([np.asarray(r['out'], np.float32) for r in res.results], axis=0)
```

```python
import numpy as np
import concourse.bass as bass
import concourse.mybir as mybir
from concourse.bass_utils import run_bass_kernel_spmd

F32 = mybir.dt.float32
BF16 = mybir.dt.bfloat16
AF = mybir.ActivationFunctionType
ALU = mybir.AluOpType

SAME_ENGINE_SYNC = True
N_DMA_SEMS = 40


def _region(ap):
    t = ap.tensor
    pat = ap.ap
    off = ap.offset
    space = str(ap.space)
    if 'DRAM' in space.upper() or 'HBM' in space.upper() or type(t).__name__.startswith('DRam'):
        ext = 1
        for st, cnt in pat:
            ext += (cnt - 1) * abs(st)
        return (t.name, 0, 1, off, off + ext)
    if type(t).__name__.startswith('PSum'):
        return (t.name, 0, 128, 0, 1 << 30)
    pstep, pcnt = pat[0]
    if pstep == 0:
        pstep = 1 << 60
    tshape = t.shape
    per_part = 1
    for s in tshape[1:]:
        per_part *= s
    p0 = off // per_part
    f0 = off % per_part
    ext = 1
    for st, cnt in pat[1:]:
        ext += (cnt - 1) * abs(st)
    return (t.name, p0, p0 + pcnt, f0, f0 + ext)


def _overlap(a, b):
    return a[1] < b[2] and b[1] < a[2] and a[3] < b[4] and b[3] < a[4]


def _contains(a, b):
    return a[1] <= b[1] and a[2] >= b[2] and a[3] <= b[3] and a[4] >= b[4]


class Prog:
    ENG = ('pe', 'dve', 'act', 'pool', 'sp')
    G = {}

    def __init__(self, nc):
        self.nc = nc
        self.ins = []
        self.acc = {}
        self.nbuf = 0
        from contextlib import ExitStack
        self.st = ExitStack()
        Prog.NPROG = getattr(Prog, 'NPROG', 0) + 1
        self.pid = Prog.NPROG

    def sb(self, shape, dtype, name=None):
        self.nbuf += 1
        return self.st.enter_context(self.nc.sbuf_tensor(f"p{self.pid}_" + (name or f"sb{self.nbuf}"), list(shape), dtype))

    def ps(self, shape, dtype=F32, name=None):
        self.nbuf += 1
        return self.nc.alloc_psum_tensor(name or f"ps{self.nbuf}", list(shape), dtype)

    def dram(self, name, shape, dtype, kind="Internal"):
        return self.nc.dram_tensor(name, list(shape), dtype, kind=kind).ap()

    def add(self, eng, fn, reads, writes, dma=False):
        idx = len(self.ins)
        deps = set()
        eid = ('dma', idx) if dma else eng
        psr = [ap for ap in reads if type(ap.tensor).__name__.startswith('PSum')]
        if psr:
            reads = [ap for ap in reads if not type(ap.tensor).__name__.startswith('PSum')]
            writes = list(writes) + psr
        for ap in reads:
            r = _region(ap)
            a = self.acc.setdefault(r[0], {'w': [], 'r': []})
            for (wr, wi, we) in a['w']:
                if _overlap(wr, r):
                    deps.add(wi)
            a['r'] = [(rr, ri, re) for (rr, ri, re) in a['r'] if not (re == eid and _contains(r, rr))]
            a['r'].append((r, idx, eid))
        for ap in writes:
            w = _region(ap)
            a = self.acc.setdefault(w[0], {'w': [], 'r': []})
            for (wr, wi, we) in a['w']:
                if _overlap(wr, w):
                    deps.add(wi)
            for (rr, ri, re) in a['r']:
                if _overlap(rr, w) and ri != idx:
                    deps.add(ri)
            a['w'] = [(wr, wi, we) for (wr, wi, we) in a['w'] if not _contains(w, wr)]
            a['r'] = [(rr, ri, re) for (rr, ri, re) in a['r'] if not (_contains(w, rr) and ri != idx)]
            a['w'].append((w, idx, eid))
        deps.discard(idx)
        self.ins.append(dict(eng=eng, fn=fn, deps=deps, dma=dma))
        return idx

    def finish(self):
        self.emit()
        self.st.close()
        self.ins = []
        self.acc = {}

    def emit(self):
        nc = self.nc
        ins = self.ins
        n = len(ins)
        G = getattr(nc, '_mk_G', None) if False else Prog.G.get(id(nc))
        if G is None:
            G = dict(esem={e: nc.semaphore(f"sg_{e}").__enter__() for e in self.ENG},
                     dsem=[nc.semaphore(f"sg_dma{k}").__enter__() for k in range(N_DMA_SEMS)],
                     cnt={e: 0 for e in self.ENG}, dma_count=[0] * N_DMA_SEMS, dma_next=0)
            Prog.G[id(nc)] = G
        esem, dsem = G['esem'], G['dsem']
        start_cnt = dict(G['cnt'])
        start_dma = list(G['dma_count'])
        need = [False] * n
        for i, it in enumerate(ins):
            for d in it['deps']:
                de = ins[d]
                if de['dma']:
                    need[d] = True
                elif de['eng'] != it['eng'] or it['dma']:
                    need[d] = True
                elif SAME_ENGINE_SYNC and it['eng'] != 'pe':
                    need[d] = True
        last_of = {}
        for i, it in enumerate(ins):
            if not it['dma']:
                last_of[it['eng']] = i
        for e, i in last_of.items():
            need[i] = True
        cnt = G['cnt']
        dma_count = G['dma_count']
        ms = [None] * n
        dma_wait_before = [None] * n
        for i, it in enumerate(ins):
            if it['dma']:
                s_ = G['dma_next'] % N_DMA_SEMS
                G['dma_next'] += 1
                if dma_count[s_] > 0:
                    dma_wait_before[i] = (s_, dma_count[s_])
                dma_count[s_] += 16
                ms[i] = ('dma', s_, dma_count[s_])
            elif need[i]:
                cnt[it['eng']] += 1
                ms[i] = ('eng', it['eng'], cnt[it['eng']])
        end_dma = list(dma_count)
        with nc.Block() as block:

            def run(engname, e):
                known = {}

                def wait_for(m):
                    kind, key, val = m
                    if known.get((kind, key), 0) >= val:
                        return
                    sem = esem[key] if kind == 'eng' else dsem[key]
                    e.wait_ge(sem, val)
                    known[(kind, key)] = val

                for e2 in self.ENG:
                    if start_cnt[e2] > 0 and e2 != engname:
                        wait_for(('eng', e2, start_cnt[e2]))
                for s_ in range(N_DMA_SEMS):
                    if start_dma[s_] > 0:
                        wait_for(('dma', s_, start_dma[s_]))
                if start_cnt[engname] > 0:
                    known[('eng', engname)] = start_cnt[engname]
                for i, it in enumerate(ins):
                    if it['eng'] != engname:
                        continue
                    for d in sorted(it['deps']):
                        de = ins[d]
                        if ms[d] is None:
                            continue
                        if (not de['dma']) and de['eng'] == engname and not it['dma']:
                            if not (SAME_ENGINE_SYNC and engname != 'pe'):
                                continue
                        wait_for(ms[d])
                    if it['dma'] and dma_wait_before[i] is not None:
                        wait_for(('dma',) + dma_wait_before[i])
                    r = it['fn'](e)
                    if ms[i] is not None:
                        kind, key, val = ms[i]
                        if kind == 'dma':
                            r.then_inc(dsem[key], 16)
                        else:
                            r.then_inc(esem[key], 1)
                if engname == 'sp':
                    for s_ in range(N_DMA_SEMS):
                        if end_dma[s_] > 0:
                            wait_for(('dma', s_, end_dma[s_]))

            @block.tensor
            def _(e):
                run('pe', e)

            @block.vector
            def _(e):
                run('dve', e)

            @block.scalar
            def _(e):
                run('act', e)

            @block.gpsimd
            def _(e):
                run('pool', e)

            @block.sync
            def _(e):
                run('sp', e)

    def dma(self, out, in_, eng='sp', **kw):
        return self.add(eng, lambda e: e.dma_start(out=out, in_=in_, **kw), [in_], [out], dma=True)

    def mm(self, out, lhsT, rhs, start=True, stop=True, **kw):
        return self.add('pe', lambda e: e.matmul(out, lhsT, rhs, start=start, stop=stop, **kw), [lhsT, rhs] + ([] if start else [out]), [out])

    def transpose(self, out, in_, ident):
        return self.add('pe', lambda e: e.transpose(out, in_, ident), [in_, ident], [out])

    def act(self, out, in_, func, bias=None, scale=None, eng='act'):
        kw = {}
        rd = [in_]
        if bias is not None:
            kw['bias'] = bias
            if not isinstance(bias, (int, float)):
                rd.append(bias)
        if scale is not None:
            kw['scale'] = scale
            if not isinstance(scale, (int, float)):
                rd.append(scale)
        return self.add(eng, lambda e: e.activation(out=out, in_=in_, func=func, **kw), rd, [out])

    def tt(self, out, in0, in1, op, eng='dve'):
        return self.add(eng, lambda e: e.tensor_tensor(out=out, in0=in0, in1=in1, op=op), [in0, in1], [out])

    def ts(self, out, in0, s1, op0, s2=None, op1=None, eng='dve'):
        rd = [in0]
        if not isinstance(s1, (int, float)):
            rd.append(s1)
        if s2 is not None and not isinstance(s2, (int, float)):
            rd.append(s2)
        if op1 is None:
            return self.add(eng, lambda e: e.tensor_scalar(out=out, in0=in0, scalar1=s1, scalar2=None, op0=op0), rd, [out])
        return self.add(eng, lambda e: e.tensor_scalar(out=out, in0=in0, scalar1=s1, scalar2=s2, op0=op0, op1=op1), rd, [out])

    def stt(self, out, in0, scalar, in1, op0, op1, eng='dve'):
        rd = [in0, in1]
        if not isinstance(scalar, (int, float)):
            rd.append(scalar)
        return self.add(eng, lambda e: e.scalar_tensor_tensor(out=out, in0=in0, scalar=scalar, in1=in1, op0=op0, op1=op1), rd, [out])

    def copy(self, out, in_, eng='dve'):
        if eng == 'act':
            return self.add(eng, lambda e: e.copy(out=out, in_=in_), [in_], [out])
        return self.add(eng, lambda e: e.tensor_copy(out=out, in_=in_), [in_], [out])

    def memset(self, ap, val, eng='dve'):
        return self.add(eng, lambda e: e.memset(ap, val), [], [ap])

    def recip(self, out, in_):
        return self.add('dve', lambda e: e.reciprocal(out=out, in_=in_), [in_], [out])
D = 1024
KC = 8
FH = 2816
EPS = 1e-6


class Rot:
    def __init__(self, items):
        self.items = items
        self.i = 0

    def next(self):
        r = self.items[self.i % len(self.items)]
        self.i += 1
        return r


def make_consts():
    c = {}
    c['ident'] = np.eye(128, dtype=np.float32)
    c['ones'] = np.ones((128, 128), np.float32)
    t = np.arange(128)
    c['ms'] = (t[:, None] > t[None, :]).astype(np.float32)
    c['ule'] = (t[:, None] <= t[None, :]).astype(np.float32)
    c['ones2'] = np.ones((128, 128), np.float32)
    c['negtri'] = np.where(t[None, :] < t[:, None], -30000.0, 0.0).astype(np.float32)
    c['negtri_s'] = np.where(t[None, :] >= t[:, None], -30000.0, 0.0).astype(np.float32)
    c['zeros'] = np.zeros((128, 128), np.float32)
    names = list(c.keys())
    arr = np.concatenate([c[n] for n in names], axis=1)
    offs = {n: i * 128 for i, n in enumerate(names)}
    return arr, offs


class Ctx:
    def __init__(self, P, consts_ap, offs):
        self.P = P
        nc = P.nc
        ncol = consts_ap.shape[1]
        self.c32 = P.nc.alloc_sbuf_tensor("c32", [128, ncol], F32)
        self.c16 = P.nc.alloc_sbuf_tensor("c16", [128, ncol], BF16)
        P.dma(self.c32[:], consts_ap)
        P.copy(self.c16[:], self.c32[:])
        self.offs = offs
        self.banks = Rot([P.ps([128, 512], F32, name=f"bank{i}") for i in range(8)])
        self.epsb = P.nc.alloc_sbuf_tensor("epsb", [128, 1], F32)
        P.memset(self.epsb[:], EPS)
        self.oneb = P.nc.alloc_sbuf_tensor("oneb", [128, 1], F32)
        P.memset(self.oneb[:], 1.0)

    def k32(self, name, p=128, n=128):
        o = self.offs[name]
        return self.c32[0:p, o:o + n]

    def k16(self, name, p=128, n=128):
        o = self.offs[name]
        return self.c16[0:p, o:o + n]

    def bank(self):
        return self.banks.next()


def rstd_from_sumsq(cx, out_rstd, ss_psum, n, tmp):
    P = cx.P
    P.act(tmp, ss_psum, AF.Sqrt, bias=cx.epsb[:], scale=1.0 / n)
    P.recip(out_rstd, tmp)


def rmsnorm_tile(cx, xt, w_sb, h_out, TT, sq, rstd, tmp):
    P = cx.P
    bank = cx.bank()
    for k in range(KC):
        P.act(sq[:, k, 0:TT], xt[:, k, 0:TT], AF.Square)
    for k in range(KC):
        P.mm(bank[:, 0:TT], cx.k16('ones'), sq[:, k, 0:TT], start=(k == 0), stop=(k == KC - 1))
    rstd_from_sumsq(cx, rstd[:, 0:TT], bank[:, 0:TT], D, tmp[:, 0:TT])
    for k in range(KC):
        P.stt(h_out[:, k, 0:TT], xt[:, k, 0:TT], w_sb[:, k:k + 1], rstd[:, 0:TT], ALU.mult, ALU.mult)


def phase_load(cx, x_in, xT, NS, L):
    P = cx.P
    xin = Rot([P.sb([128, D], F32, name=f"ld_in{i}") for i in range(3)])
    xo = Rot([P.sb([128, KC, 512], F32, name=f"ld_o{i}") for i in range(2)])
    xTv = xT.rearrange("(k p) t -> p k t", p=128)
    for s in range(NS):
        for b4 in range(L // 512):
            ot = xo.next()
            for bb in range(4):
                l0 = b4 * 512 + bb * 128
                it = xin.next()
                P.dma(it[:], x_in[s, l0:l0 + 128, :])
                for half in range(2):
                    bank = cx.bank()
                    for kk in range(4):
                        k = half * 4 + kk
                        P.transpose(bank[:, kk * 128:(kk + 1) * 128], it[:, k * 128:(k + 1) * 128], cx.k32('ident'))
                    src = bank[:, :].rearrange("p (k t) -> p k t", k=4)
                    eng = 'act' if half == 0 else 'dve'
                    P.copy(ot[:, half * 4:half * 4 + 4, bb * 128:(bb + 1) * 128], src, eng=eng)
            t0 = s * L + b4 * 512
            P.dma(xTv[:, :, t0:t0 + 512], ot[:])


def phase_final(cx, xT, w_ap, out, NS, L):
    P = cx.P
    TT = 512
    w_sb = P.sb([128, KC], F32, name="fin_w")
    P.dma(w_sb[:], w_ap)
    xt = Rot([P.sb([128, KC, TT], F32, name=f"fin_x{i}") for i in range(2)])
    hs = Rot([P.sb([128, KC, TT], F32, name=f"fin_h{i}") for i in range(2)])
    sq = P.sb([128, KC, TT], BF16, name="fin_sq")
    rstd = P.sb([128, TT], F32, name="fin_rstd")
    tmp = P.sb([128, TT], F32, name="fin_tmp")
    ob = Rot([P.sb([128, D], F32, name=f"fin_o{i}") for i in range(3)])
    xTv = xT.rearrange("(k p) t -> p k t", p=128)
    for s in range(NS):
        for b4 in range(L // TT):
            t0 = s * L + b4 * TT
            x = xt.next()
            h = hs.next()
            P.dma(x[:], xTv[:, :, t0:t0 + TT])
            rmsnorm_tile(cx, x, w_sb, h, TT, sq, rstd, tmp)
            for bb in range(4):
                o = ob.next()
                for half in range(2):
                    bank = cx.bank()
                    for kk in range(4):
                        k = half * 4 + kk
                        P.transpose(bank[:, kk * 128:(kk + 1) * 128], h[:, k, bb * 128:(bb + 1) * 128], cx.k32('ident'))
                    P.copy(o[:, half * 512:(half + 1) * 512], bank[:, :], eng='act' if half == 0 else 'dve')
                l0 = b4 * TT + bb * 128
                P.dma(out[s, l0:l0 + 128, :], o[:])


def load_w_bf16(cx, dst, src_ap, rows, cols, col0=0, ncols=None):
    P = cx.P
    ncols = cols if ncols is None else ncols
    v = src_ap.rearrange("(k p) n -> p k n", p=128)
    kc = rows // 128
    if not hasattr(P, 'wstage'):
        P.wstage = Rot([P.sb([128, 1024], F32, name=f"wstage{i}") for i in range(2)])
    for k in range(kc):
        for c0 in range(0, ncols, 1024):
            c1 = min(ncols, c0 + 1024)
            st = P.wstage.next()
            P.dma(st[:, 0:c1 - c0], v[:, k, col0 + c0:col0 + c1])
            P.copy(dst[:, k, c0:c1], st[:, 0:c1 - c0], eng='pool')


def phase_ffn(cx, xT, norm_w, w_in, w_out, T):
    P = cx.P
    TT = 256
    HC = FH // 128
    win = P.sb([128, KC, 2 * FH], BF16, name="ffn_win")
    wout = P.sb([128, HC, D], BF16, name="ffn_wout")
    load_w_bf16(cx, win, w_in, D, 2 * FH)
    load_w_bf16(cx, wout, w_out, FH, D)
    w_sb = P.sb([128, KC], F32, name="ffn_nw")
    P.dma(w_sb[:], norm_w)
    xt = Rot([P.sb([128, KC, TT], F32, name=f"ffn_x{i}") for i in range(2)])
    hb = Rot([P.sb([128, KC, TT], BF16, name=f"ffn_h{i}") for i in range(2)])
    sq = P.sb([128, KC, TT], BF16, name="ffn_sq")
    rstd = P.sb([128, TT], F32, name="ffn_rstd")
    tmp = P.sb([128, TT], F32, name="ffn_tmp")
    sg = Rot([P.sb([128, TT], F32, name=f"ffn_sg{i}") for i in range(3)])
    actT = Rot([P.sb([128, HC, TT], BF16, name=f"ffn_act{i}") for i in range(1)])
    xTv = xT.rearrange("(k p) t -> p k t", p=128)
    DBG = 9
    for ti in range(T // TT):
        if DBG < 2:
            break
        t0 = ti * TT
        x = xt.next()
        h = hb.next()
        a = actT.next()
        P.dma(x[:], xTv[:, :, t0:t0 + TT])
        rmsnorm_tile(cx, x, w_sb, h, TT, sq, rstd, tmp)
        for c in range(HC):
            if DBG < 3:
                break
            bg = cx.bank()
            bu = cx.bank()
            for k in range(KC):
                P.mm(bg[:, 0:TT], win[:, k, c * 128:(c + 1) * 128], h[:, k, :], start=(k == 0), stop=(k == KC - 1))
            for k in range(KC):
                P.mm(bu[:, 0:TT], win[:, k, FH + c * 128:FH + (c + 1) * 128], h[:, k, :], start=(k == 0), stop=(k == KC - 1))
            s_ = sg.next()
            P.act(s_[:], bg[:, 0:TT], AF.Silu)
            P.tt(a[:, c, :], s_[:], bu[:, 0:TT], ALU.mult)
        for m in range(KC):
            if DBG < 4:
                break
            bo = cx.bank()
            for c in range(HC):
                P.mm(bo[:, 0:TT], wout[:, c, m * 128:(m + 1) * 128], a[:, c, :], start=(c == 0), stop=(c == HC - 1))
            P.tt(x[:, m, :], x[:, m, :], bo[:, 0:TT], ALU.add)
        P.dma(xTv[:, :, t0:t0 + TT], x[:])
def gdn_core(cx, j):
    P = cx.P
    A = cx.A
    NS, L, T = cx.NS, cx.L, cx.T
    CH = 256
    NCH = L // CH
    hTv = cx.hT.rearrange("(k p) t -> p k t", p=128)
    yv = cx.yT.rearrange("(c p) t -> p c t", p=128)
    winv = A['gdn_w_in'][j].rearrange("(k p) n -> p k n", p=128)
    cw = P.sb([128, 32, 4], F32, name="g_cw")
    P.dma(cw[:], A['gdn_convw'][j])
    nw = P.sb([128, 2], F32, name="g_nw")
    P.dma(nw[:], A['gdn_normw'][j])
    dtb = P.sb([1, 8], F32, name="g_dtb")
    P.dma(dtb[:], A['gdn_dtb'][j])
    aneg = P.sb([1, 8], F32, name="g_aneg")
    P.dma(aneg[:], A['gdn_alog'][j])
    P.act(aneg[:], aneg[:], AF.Exp)
    P.ts(aneg[:], aneg[:], -1.0, ALU.mult)
    ident32 = cx.k32('ident'); ones32 = cx.k32('ones'); ule = cx.k32('ule'); ms = cx.k32('ms')
    negtri = cx.k32('negtri'); negtri_s = cx.k32('negtri_s')
    S1 = lambda n, shp, dt_=F32: P.sb(shp, dt_, name="g_" + n)
    wst = Rot([S1(f'wst{i}', [128, KC, 256]) for i in range(2)])
    wg = Rot([S1(f'wg{i}', [128, KC, 770], BF16) for i in range(2)])
    def mk(sfx):
        S1 = lambda n, shp, dt_=F32: P.sb(shp, dt_, name="g_" + n + sfx)
        hbuf = Rot([S1(f'hb{i}', [128, KC, CH], BF16) for i in range(2)])
        pc = S1('pc', [128, 4, CH + 3]); sgate = S1('sgate', [128, 2, CH])
        bgT = S1('bgT', [1, 2, CH]); te = S1('te', [1, CH])
        cacc = Rot([S1(f'cacc{i}', [128, CH]) for i in range(2)])
        cvf = S1('cvf', [128, 4, CH]); sq = S1('sq', [128, 2, CH], BF16)
        rn = S1('rn', [128, CH]); tmp = S1('tmp', [128, CH])
        qkb = S1('qkb', [128, 2, CH], BF16)
        ktok = S1('ktok', [128, 128]); bgtok = S1('bgtok', [128, 2]); vb = S1('vb', [128, 256], BF16)
        gcol = S1('gcol', [128, 4])
        Gm = S1('Gm', [128, 128]); AO = S1('AO', [128, 128])
        decT = S1('decT', [128, 128], BF16); dec = S1('dec', [128, 128]); Erow = S1('Erow', [128, 128])
        Nk = Rot([S1(f'Nk{i}', [128, 128]) for i in range(2)])
        Ak = Rot([S1(f'Ak{i}', [128, 128]) for i in range(2)])
        Q = Rot([S1(f'Q{i}', [128, 128]) for i in range(2)])
        TTb = S1('TTb', [128, 128], BF16); Rk = S1('Rk', [128, 128], BF16); nwT = S1('nwT', [128, 128], BF16)
        vnew = S1('vnew', [128, 256], BF16); attT = S1('attT', [128, 128], BF16); qe = S1('qe', [128, 128], BF16)
        kdec = S1('kdec', [128, 128], BF16)
        oT = S1('oT', [128, 2, CH]); osq = S1('osq', [128, 2, CH], BF16); o2 = S1('o2', [128, 2, CH])
        ob = Rot([S1(f'ob{i}', [128, 2, CH], BF16) for i in range(2)])
        S32 = S1('S32', [128, 256]); Sb = S1('Sb', [128, 256], BF16)
        return dict(hbuf=hbuf, pc=pc, sgate=sgate, bgT=bgT, te=te, cacc=cacc, cvf=cvf, sq=sq, rn=rn, tmp=tmp, qkb=qkb, ktok=ktok, bgtok=bgtok, vb=vb, gcol=gcol, Gm=Gm, AO=AO, decT=decT, dec=dec, Erow=Erow, Nk=Nk, Ak=Ak, Q=Q, TTb=TTb, Rk=Rk, nwT=nwT, vnew=vnew, attT=attT, qe=qe, kdec=kdec, oT=oT, osq=osq, o2=o2, ob=ob, S32=S32, Sb=Sb)
    BF = [mk(f"_s{s}") for s in range(NS)]

    def chain(s, w, chs, h, Bf):
        hbuf = Bf['hbuf']
        pc = Bf['pc']
        sgate = Bf['sgate']
        bgT = Bf['bgT']
        te = Bf['te']
        cacc = Bf['cacc']
        cvf = Bf['cvf']
        sq = Bf['sq']
        rn = Bf['rn']
        tmp = Bf['tmp']
        qkb = Bf['qkb']
        ktok = Bf['ktok']
        bgtok = Bf['bgtok']
        vb = Bf['vb']
        gcol = Bf['gcol']
        Gm = Bf['Gm']
        AO = Bf['AO']
        decT = Bf['decT']
        dec = Bf['dec']
        Erow = Bf['Erow']
        Nk = Bf['Nk']
        Ak = Bf['Ak']
        Q = Bf['Q']
        TTb = Bf['TTb']
        Rk = Bf['Rk']
        nwT = Bf['nwT']
        vnew = Bf['vnew']
        attT = Bf['attT']
        qe = Bf['qe']
        kdec = Bf['kdec']
        oT = Bf['oT']
        osq = Bf['osq']
        o2 = Bf['o2']
        ob = Bf['ob']
        S32 = Bf['S32']
        Sb = Bf['Sb']
        P.memset(S32[:], 0.0)
        yield
        P.memset(Sb[:], 0.0)
        yield
        P.memset(pc[:, :, 0:3], 0.0)
        yield
        for c in range(NCH):
            t0 = s * L + c * CH
            hb = hbuf.next()
            P.dma(hb[:], hTv[:, :, t0:t0 + CH])
            yield
            for i in range(4):
                bk = cx.bank()
                for k in range(KC):
                    P.mm(bk[:, 0:CH], w[:, k, 128 * i:128 * (i + 1)], hb[:, k, :], start=(k == 0), stop=(k == KC - 1))
                    yield
                P.copy(pc[:, i, 3:3 + CH], bk[:, 0:CH], eng=('act' if i % 2 else 'dve'))
                yield
            for i in range(2):
                bk = cx.bank()
                for k in range(KC):
                    P.mm(bk[:, 0:CH], w[:, k, 512 + 128 * i:512 + 128 * (i + 1)], hb[:, k, :], start=(k == 0), stop=(k == KC - 1))
                    yield
                P.act(sgate[:, i, :], bk[:, 0:CH], AF.Silu)
                yield
            bkb = cx.bank()
            for k in range(KC):
                P.mm(bkb[0:1, 0:CH], w[:, k, 768:769], hb[:, k, :], start=(k == 0), stop=(k == KC - 1))
                yield
            P.act(bgT[:, 0, :], bkb[0:1, 0:CH], AF.Sigmoid)
            yield
            bka = cx.bank()
            for k in range(KC):
                P.mm(bka[0:1, 0:CH], w[:, k, 769:770], hb[:, k, :], start=(k == 0), stop=(k == KC - 1))
                yield
            P.act(te[:], bka[0:1, 0:CH], AF.Exp, bias=dtb[0:1, h:h + 1])
            yield
            P.act(te[:], te[:], AF.Ln, bias=cx.oneb[0:1, :])
            yield
            P.ts(bgT[:, 1, :], te[:], aneg[0:1, h:h + 1], ALU.mult)
            yield
            for i in range(4):
                acc = cacc.next()
                ch = chs[i]
                P.ts(acc[:], pc[:, i, 0:CH], cw[:, ch, 0:1], ALU.mult)
                yield
                for k in range(1, 4):
                    P.stt(acc[:], pc[:, i, k:k + CH], cw[:, ch, k:k + 1], acc[:], ALU.mult, ALU.add)
                    yield
                P.act(cvf[:, i, :], acc[:], AF.Silu)
                yield
            P.copy(pc[:, :, 0:3], pc[:, :, CH:CH + 3])
            yield
            for i in range(2):
                P.act(sq[:, i, :], cvf[:, i, :], AF.Square)
                yield
                bk = cx.bank()
                P.mm(bk[:, 0:CH], cx.k16('ones'), sq[:, i, :], start=True, stop=True)
                yield
                rstd_from_sumsq(cx, rn[:], bk[:, 0:CH], 1, tmp[:])
                if i == 0:
                    P.stt(cvf[:, 0, :], cvf[:, 0, :], 128 ** -0.5, rn[:], ALU.mult, ALU.mult)
                    yield
                else:
                    P.tt(cvf[:, 1, :], cvf[:, 1, :], rn[:], ALU.mult)
                    yield
            P.copy(qkb[:], cvf[:, 0:2, :])
            yield
            for b in range(2):
                tb = slice(128 * b, 128 * (b + 1))
                bk = cx.bank()
                P.transpose(bk[:, 0:128], cvf[:, 1, tb], ident32)
                yield
                P.transpose(bk[:, 128:256], cvf[:, 2, tb], ident32)
                yield
                P.transpose(bk[:, 256:384], cvf[:, 3, tb], ident32)
                yield
                P.transpose(bk[:, 384:385], bgT[0:1, 0, tb], cx.k32('ident', 1, 1))
                yield
                P.transpose(bk[:, 385:386], bgT[0:1, 1, tb], cx.k32('ident', 1, 1))
                yield
                P.copy(bgtok[:], bk[:, 384:386])
                yield
                P.copy(ktok[:], bk[:, 0:128], eng='act')
                yield
                P.ts(vb[:], bk[:, 128:384], bgtok[:, 0:1], ALU.mult)
                yield
                bc = cx.bank()
                P.mm(bc[:, 0:1], ule, bgtok[:, 1:2], start=True, stop=True)
                yield
                P.copy(gcol[:, 0:1], bc[:, 0:1])
                yield
                P.act(gcol[:, 1:2], gcol[:, 0:1], AF.Exp)
                yield
                P.tt(gcol[:, 2:3], gcol[:, 1:2], bgtok[:, 0:1], ALU.mult)
                yield
                P.ts(gcol[:, 3:4], bgtok[:, 0:1], -1.0, ALU.mult)
                yield
                P.ts(Gm[:], ms, bgtok[:, 1:2], ALU.mult)
                yield
                P.ts(AO[:], ones32, bgtok[:, 1:2], ALU.mult, eng='pool')
                yield
                bsT = cx.bank()
                P.mm(bsT[:, 0:128], Gm[:], ule, start=True, stop=False)
                yield
                P.mm(bsT[:, 0:128], ident32, negtri, start=False, stop=True)
                yield
                P.act(decT[:], bsT[:, 0:128], AF.Exp)
                yield
                bs_ = cx.bank()
                P.mm(bs_[:, 0:128], ule, Gm[:], start=True, stop=False)
                yield
                P.mm(bs_[:, 0:128], ident32, negtri_s, start=False, stop=True)
                yield
                P.act(dec[:], bs_[:, 0:128], AF.Exp)
                yield
                be = cx.bank()
                P.mm(be[:, 0:128], AO[:], ule, start=True, stop=True)
                yield
                P.act(Erow[:], be[:, 0:128], AF.Exp)
                yield
                bkk = cx.bank()
                P.mm(bkk[:, 0:128], qkb[:, 1, tb], qkb[:, 1, tb], start=True, stop=True)
                yield
                n_ = Nk.next(); a_ = Ak.next(); q_ = Q.next()
                P.stt(n_[:], bkk[:, 0:128], gcol[:, 3:4], dec[:], ALU.mult, ALU.mult)
                yield
                bt = cx.bank()
                P.transpose(bt[:, 0:128], n_[:], ident32)
                yield
                P.copy(a_[:], bt[:, 0:128], eng='act')
                yield
                P.tt(q_[:], bt[:, 0:128], ident32, ALU.add)
                yield
                for lvl in range(6):
                    n2 = Nk.next(); a2 = Ak.next(); q2 = Q.next()
                    b1 = cx.bank()
                    P.mm(b1[:, 0:128], a_[:], n_[:], start=True, stop=True)
                    P.copy(n2[:], b1[:, 0:128], eng='act')
                    yield
                    if lvl < 5:
                        b2 = cx.bank()
                        P.mm(b2[:, 0:128], n_[:], a_[:], start=True, stop=True)
                        P.copy(a2[:], b2[:, 0:128])
                        yield
                    b3 = cx.bank()
                    P.mm(b3[:, 0:128], n2[:], q_[:], start=True, stop=True)
                    P.tt(q2[:], q_[:], b3[:, 0:128], ALU.add)
                    yield
                    n_, a_, q_ = n2, a2, q2
                P.copy(TTb[:], q_[:], eng='act')
                yield
                P.ts(Rk[:], ktok[:], gcol[:, 2:3], ALU.mult)
                yield
                bw = cx.bank()
                P.mm(bw[:, 0:128], Rk[:], TTb[:], start=True, stop=True)
                yield
                P.ts(nwT[:], bw[:, 0:128], -1.0, ALU.mult)
                yield
                bv = cx.bank()
                P.mm(bv[:, 0:256], TTb[:], vb[:], start=True, stop=False)
                yield
                P.mm(bv[:, 0:256], nwT[:], Sb[:], start=False, stop=True)
                yield
                P.copy(vnew[:], bv[:, 0:256], eng='act')
                yield
                bq = cx.bank()
                P.mm(bq[:, 0:128], qkb[:, 1, tb], qkb[:, 0, tb], start=True, stop=True)
                yield
                P.tt(attT[:], bq[:, 0:128], decT[:], ALU.mult)
                yield
                P.tt(qe[:], cvf[:, 0, tb], Erow[:], ALU.mult, eng='pool')
                yield
                bo = cx.bank()
                for e in range(2):
                    P.mm(bo[:, 128 * e:128 * (e + 1)], Sb[:, 128 * e:128 * (e + 1)], qe[:], start=True, stop=False)
                    yield
                    P.mm(bo[:, 128 * e:128 * (e + 1)], vnew[:, 128 * e:128 * (e + 1)], attT[:], start=False, stop=True)
                    yield
                P.copy(oT[:, :, tb], bo[:, 0:256].rearrange("p (e t) -> p e t", e=2), eng='act')
                yield
                P.ts(kdec[:], ktok[:], decT[:, 127:128], ALU.mult)
                yield
                bS = cx.bank()
                P.mm(bS[:, 0:256], kdec[:], vnew[:], start=True, stop=True)
                yield
                P.stt(S32[:], S32[:], Erow[:, 127:128], bS[:, 0:256], ALU.mult, ALU.add)
                yield
                P.copy(Sb[:], S32[:], eng='act')
                yield
            for e in range(2):
                P.act(osq[:, e, :], oT[:, e, :], AF.Square)
                yield
            bn = cx.bank()
            for e in range(2):
                P.mm(bn[:, 0:CH], cx.k16('ones'), osq[:, e, :], start=(e == 0), stop=(e == 1))
                yield
            rstd_from_sumsq(cx, rn[:], bn[:, 0:CH], 256, tmp[:])
            o_ = ob.next()
            for e in range(2):
                P.stt(o2[:, e, :], oT[:, e, :], nw[:, e:e + 1], rn[:], ALU.mult, ALU.mult)
                yield
                P.tt(o_[:, e, :], o2[:, e, :], sgate[:, e, :], ALU.mult, eng='pool')
                yield
            P.dma(yv[:, 2 * h:2 * h + 2, t0:t0 + CH], o_[:])
            yield


    for h in range(8):
        w = wg.next()
        for (c0, n, d0) in [(128 * h, 128, 0), (1024 + 128 * h, 128, 128), (2048 + 256 * h, 256, 256),
                            (4096 + 256 * h, 256, 512), (6144 + h, 1, 768), (6152 + h, 1, 769)]:
            st = wst.next()
            P.dma(st[:, :, 0:n], winv[:, :, c0:c0 + n], **({'allow_slow_non_contiguous': True} if n == 1 else {}))
            P.copy(w[:, :, d0:d0 + n], st[:, :, 0:n], eng='pool')
        chs = [h, 8 + h, 16 + 2 * h, 17 + 2 * h]
        gens = [chain(s, w, chs, h, BF[s]) for s in range(NS)]
        while gens:
            for g_ in list(gens):
                try:
                    next(g_)
                except StopIteration:
                    gens.remove(g_)


def phase_gdn(cx, xT, i_layer, j):
    nc = cx.P.nc
    phase_prenorm(cx, xT, cx.A['mix_norm_w'][i_layer], cx.hT)
    cx.P.finish()
    cx.P = Prog(nc)
    gdn_core(cx, j)
    cx.P.finish()
    cx.P = Prog(nc)
    phase_outproj(cx, xT, cx.yT, cx.A['gdn_w_out'][j], 2048)
def phase_prenorm(cx, xT, w_ap, hT):
    P = cx.P
    TT = 512
    T = cx.T
    w_sb = P.sb([128, KC], F32, name="pn_w")
    P.dma(w_sb[:], w_ap)
    xt = Rot([P.sb([128, KC, TT], F32, name=f"pn_x{i}") for i in range(2)])
    hs = Rot([P.sb([128, KC, TT], BF16, name=f"pn_h{i}") for i in range(2)])
    sq = P.sb([128, KC, TT], BF16, name="pn_sq")
    rstd = P.sb([128, TT], F32, name="pn_rstd")
    tmp = P.sb([128, TT], F32, name="pn_tmp")
    xTv = xT.rearrange("(k p) t -> p k t", p=128)
    hTv = hT.rearrange("(k p) t -> p k t", p=128)
    for ti in range(T // TT):
        t0 = ti * TT
        x = xt.next()
        h = hs.next()
        P.dma(x[:], xTv[:, :, t0:t0 + TT])
        rmsnorm_tile(cx, x, w_sb, h, TT, sq, rstd, tmp)
        P.dma(hTv[:, :, t0:t0 + TT], h[:])


def phase_outproj(cx, xT, yT, w_ap, K):
    P = cx.P
    TT = 512
    T = cx.T
    kc = K // 128
    w = P.sb([128, kc, D], BF16, name="op_w")
    load_w_bf16(cx, w, w_ap, K, D)
    xt = Rot([P.sb([128, KC, TT], F32, name=f"op_x{i}") for i in range(2)])
    ys = Rot([P.sb([128, kc, TT], BF16, name=f"op_y{i}") for i in range(2)])
    xTv = xT.rearrange("(k p) t -> p k t", p=128)
    yTv = yT.rearrange("(k p) t -> p k t", p=128)
    for ti in range(T // TT):
        t0 = ti * TT
        x = xt.next()
        y = ys.next()
        P.dma(x[:], xTv[:, :, t0:t0 + TT])
        P.dma(y[:], yTv[:, :, t0:t0 + TT])
        for m in range(KC):
            bo = cx.bank()
            for c in range(kc):
                P.mm(bo[:, 0:TT], w[:, c, m * 128:(m + 1) * 128], y[:, c, :], start=(c == 0), stop=(c == kc - 1))
            P.tt(x[:, m, :], x[:, m, :], bo[:, 0:TT], ALU.add)
        P.dma(xTv[:, :, t0:t0 + TT], x[:])


def mamba_core(cx, j):
    P = cx.P
    A = cx.A
    NS, L = cx.NS, cx.L
    CH = 256
    NCH = L // CH
    hTv = cx.hT.rearrange("(k p) t -> p k t", p=128)
    y3v = cx.yT.rearrange("(c p) t -> p c t", p=128)
    winv = A['mamba_w_in'][j].rearrange("(k p) n -> p k n", p=128)
    cw = P.sb([128, 32, 4], F32, name="m_cw")
    P.dma(cw[:], A['mamba_convw'][j])
    cb = P.sb([128, 32], F32, name="m_cb")
    P.dma(cb[:], A['mamba_convb'][j])
    dcol = P.sb([128, 16], F32, name="m_dcol")
    P.dma(dcol[:], A['mamba_drep'][j])
    nw = P.sb([128, 16], F32, name="m_nw")
    P.dma(nw[:], A['mamba_normw'][j])
    dtb = P.sb([4, 8], F32, name="m_dtb")
    P.dma(dtb[:], A['mamba_dtb'][j])
    aneg = P.sb([4, 8], F32, name="m_aneg")
    P.dma(aneg[:], A['mamba_alog'][j])
    P.act(aneg[:], aneg[:], AF.Exp)
    P.ts(aneg[:], aneg[:], -1.0, ALU.mult)
    ones32 = cx.k32('ones')
    ident32 = cx.k32('ident')
    u0 = cx.c32[:, cx.offs['ule']:cx.offs['ule'] + 256]
    ule = cx.k32('ule')
    negtri = cx.k32('negtri')
    ms = cx.k32('ms')

    wg = Rot([P.sb([128, KC, 772], BF16, name=f"m_wg{i}") for i in range(2)])
    def mk(sfx):
        hbuf = Rot([P.sb([128, KC, CH], BF16, name=sfx + f"m_hb{i}") for i in range(2)])
        sz = P.sb([128, 2, CH], F32, name=sfx + "m_sz")
        pc = P.sb([128, 4, CH + 3], F32, name=sfx + "m_pc")
        da = P.sb([4, 2, CH], F32, name=sfx + "m_da")
        dte = P.sb([4, CH], F32, name=sfx + "m_dte")
        cacc = Rot([P.sb([128, CH], F32, name=sfx + f"m_cacc{i}") for i in range(2)])
        cvf = P.sb([128, 4, CH], F32, name=sfx + "m_cvf")
        bct = P.sb([128, 2, CH], BF16, name=sfx + "m_bct")
        datok = P.sb([128, 2, 8], F32, name=sfx + "m_datok")
        xtb = P.sb([128, 2, 256], BF16, name=sfx + "m_xtb")
        btok = P.sb([128, 2, 128], BF16, name=sfx + "m_btok")
        AM = P.sb([128, 2, 4, 128], F32, name=sfx + "m_AM")
        AO = P.sb([128, 2, 4, 128], F32, name=sfx + "m_AO")
        dec = P.sb([128, 4, 384], BF16, name=sfx + "m_dec")
        E = P.sb([128, 4, CH], F32, name=sfx + "m_E")
        gT = P.sb([128, 384], F32, name=sfx + "m_gT")
        Mr = Rot([P.sb([128, 384], BF16, name=sfx + f"m_Mr{i}") for i in range(2)])
        Ctl = Rot([P.sb([128, CH], BF16, name=sfx + f"m_Ct{i}") for i in range(2)])
        ysb = P.sb([128, 2, CH], F32, name=sfx + "m_ysb")
        y2 = P.sb([128, 2, CH], F32, name=sfx + "m_y2")
        ysq = P.sb([128, 2, CH], BF16, name=sfx + "m_ysq")
        rstd = P.sb([128, CH], F32, name=sfx + "m_rstd")
        tmp = P.sb([128, CH], F32, name=sfx + "m_tmp")
        y3 = Rot([P.sb([128, 2, CH], BF16, name=sfx + f"m_y3{i}") for i in range(2)])
        xw = P.sb([128, 2, 256], BF16, name=sfx + "m_xw")
        st32 = P.sb([128, 256], F32, name=sfx + "m_st32")
        stb = P.sb([128, 256], BF16, name=sfx + "m_stb")
        return dict(hbuf=hbuf, sz=sz, pc=pc, da=da, dte=dte, cacc=cacc, cvf=cvf, bct=bct, datok=datok, xtb=xtb, btok=btok, AM=AM, AO=AO, dec=dec, E=E, gT=gT, Mr=Mr, Ctl=Ctl, ysb=ysb, y2=y2, ysq=ysq, rstd=rstd, tmp=tmp, y3=y3, xw=xw, st32=st32, stb=stb)
    BF = [mk(f"s{s}_") for s in range(NS)]

    def chain(s, w, chs, g, Bf):
        hbuf = Bf['hbuf']
        sz = Bf['sz']
        pc = Bf['pc']
        da = Bf['da']
        dte = Bf['dte']
        cacc = Bf['cacc']
        cvf = Bf['cvf']
        bct = Bf['bct']
        datok = Bf['datok']
        xtb = Bf['xtb']
        btok = Bf['btok']
        AM = Bf['AM']
        AO = Bf['AO']
        dec = Bf['dec']
        E = Bf['E']
        gT = Bf['gT']
        Mr = Bf['Mr']
        Ctl = Bf['Ctl']
        ysb = Bf['ysb']
        y2 = Bf['y2']
        ysq = Bf['ysq']
        rstd = Bf['rstd']
        tmp = Bf['tmp']
        y3 = Bf['y3']
        xw = Bf['xw']
        st32 = Bf['st32']
        stb = Bf['stb']
        P.memset(st32[:], 0.0)
        yield
        P.memset(stb[:], 0.0)
        yield
        P.memset(pc[:, :, 0:3], 0.0)
        yield
        for c in range(NCH):
            t0 = s * L + c * CH
            hb = hbuf.next()
            P.dma(hb[:], hTv[:, :, t0:t0 + CH])
            yield
            for i in range(2):
                bk = cx.bank()
                for k in range(KC):
                    P.mm(bk[:, 0:CH], w[:, k, 128 * i:128 * (i + 1)], hb[:, k, :], start=(k == 0), stop=(k == KC - 1))
                    yield
                P.act(sz[:, i, :], bk[:, 0:CH], AF.Silu)
                yield
            for i in range(4):
                bk = cx.bank()
                for k in range(KC):
                    P.mm(bk[:, 0:CH], w[:, k, 256 + 128 * i:256 + 128 * (i + 1)], hb[:, k, :], start=(k == 0), stop=(k == KC - 1))
                    yield
                P.copy(pc[:, i, 3:3 + CH], bk[:, 0:CH], eng=('act' if i % 2 else 'dve'))
                yield
            bk = cx.bank()
            for k in range(KC):
                P.mm(bk[0:4, 0:CH], w[:, k, 768:772], hb[:, k, :], start=(k == 0), stop=(k == KC - 1))
                yield
            P.act(dte[:], bk[0:4, 0:CH], AF.Exp, bias=dtb[:, g:g + 1])
            yield
            P.act(da[:, 0, :], dte[:], AF.Ln, bias=cx.oneb[0:4, :])
            yield
            P.ts(da[:, 1, :], da[:, 0, :], aneg[:, g:g + 1], ALU.mult)
            yield
            for i in range(4):
                acc = cacc.next()
                ch = chs[i]
                P.ts(acc[:], pc[:, i, 0:CH], cw[:, ch, 0:1], ALU.mult)
                yield
                for k in range(1, 4):
                    P.stt(acc[:], pc[:, i, k:k + CH], cw[:, ch, k:k + 1], acc[:], ALU.mult, ALU.add)
                    yield
                P.act(cvf[:, i, :], acc[:], AF.Silu, bias=cb[:, ch:ch + 1])
                yield
            P.copy(pc[:, :, 0:3], pc[:, :, CH:CH + 3])
            yield
            P.copy(bct[:, :, :], cvf[:, 2:4, :])
            yield
            for b in range(2):
                bk = cx.bank()
                for i in range(3):
                    P.transpose(bk[:, 128 * i:128 * (i + 1)], cvf[:, i, 128 * b:128 * (b + 1)], ident32)
                    yield
                P.transpose(bk[:, 384:388], da[0:4, 0, 128 * b:128 * (b + 1)], cx.k32('ident', 4, 4))
                yield
                P.transpose(bk[:, 388:392], da[0:4, 1, 128 * b:128 * (b + 1)], cx.k32('ident', 4, 4))
                yield
                P.copy(datok[:, b, :], bk[:, 384:392])
                yield
                P.copy(btok[:, b, :], bk[:, 256:384], eng='act')
                yield
                dtbc = datok[:, b, 0:4].unsqueeze(2).to_broadcast([128, 4, 64])
                P.tt(xtb[:, b, :].rearrange("p (r q) -> p r q", r=4), bk[:, 0:256].rearrange("p (r q) -> p r q", r=4), dtbc, ALU.mult)
                yield
                for r in range(4):
                    P.ts(AM[:, b, r, :], ms, datok[:, b, 4 + r:5 + r], ALU.mult)
                    yield
                    P.ts(AO[:, b, r, :], ones32, datok[:, b, 4 + r:5 + r], ALU.mult, eng='pool')
                    yield
            bG = cx.bank()
            P.mm(bG[:, 0:256], bct[:, 0, 0:128], bct[:, 1, 0:256], start=True, stop=True)
            yield
            P.mm(bG[:, 256:384], bct[:, 0, 128:256], bct[:, 1, 128:256], start=True, stop=True)
            yield
            P.copy(gT[:], bG[:, 0:384], eng='act')
            yield
            for r in range(4):
                bs_ = cx.bank()
                P.mm(bs_[:, 0:256], AM[:, 0, r, :], u0, start=True, stop=False)
                yield
                P.mm(bs_[:, 128:256], AO[:, 1, r, :], ule, start=False, stop=False)
                yield
                P.mm(bs_[:, 0:128], ident32, negtri, start=False, stop=True)
                yield
                P.mm(bs_[:, 256:384], AM[:, 1, r, :], ule, start=True, stop=False)
                yield
                P.mm(bs_[:, 256:384], ident32, negtri, start=False, stop=True)
                yield
                P.act(dec[:, r, :], bs_[:, 0:384], AF.Exp)
                yield
                ba = cx.bank()
                P.mm(ba[:, 0:256], AO[:, 0, r, :], u0, start=True, stop=False)
                yield
                P.mm(ba[:, 128:256], AO[:, 1, r, :], ule, start=False, stop=True)
                yield
                P.act(E[:, r, :], ba[:, 0:256], AF.Exp)
                yield
            for i in range(2):
                by = cx.bank()
                for rr in range(2):
                    r = 2 * i + rr
                    m_ = Mr.next()
                    ct = Ctl.next()
                    P.tt(m_[:], gT[:], dec[:, r, :], ALU.mult)
                    yield
                    P.tt(ct[:], cvf[:, 3, :], E[:, r, :], ALU.mult, eng='pool')
                    yield
                    o = by[64 * rr:64 * rr + 64, 0:256]
                    P.mm(o, xtb[:, 0, 64 * r:64 * r + 64], m_[:, 0:256], start=True, stop=False)
                    yield
                    P.mm(by[64 * rr:64 * rr + 64, 128:256], xtb[:, 1, 64 * r:64 * r + 64], m_[:, 256:384], start=False, stop=False)
                    yield
                    P.mm(o, stb[:, 64 * r:64 * r + 64], ct[:], start=False, stop=True)
                    yield
                P.stt(ysb[:, i, :], cvf[:, i, :], dcol[:, 2 * g + i:2 * g + i + 1], by[:, 0:256], ALU.mult, ALU.add)
                yield
                P.tt(y2[:, i, :], ysb[:, i, :], sz[:, i, :], ALU.mult, eng='pool')
                yield
                P.act(ysq[:, i, :], y2[:, i, :], AF.Square)
                yield
            bq = cx.bank()
            for i in range(2):
                P.mm(bq[:, 0:CH], cx.k16('ones'), ysq[:, i, :], start=(i == 0), stop=(i == 1))
                yield
            rstd_from_sumsq(cx, rstd[:], bq[:, 0:CH], 256, tmp[:])
            yo = y3.next()
            for i in range(2):
                P.stt(yo[:, i, :], y2[:, i, :], nw[:, 2 * g + i:2 * g + i + 1], rstd[:], ALU.mult, ALU.mult)
                yield
            P.dma(y3v[:, 2 * g:2 * g + 2, t0:t0 + CH], yo[:])
            yield
            for b in range(2):
                for r in range(4):
                    col = 255 if b == 0 else 383
                    P.ts(xw[:, b, 64 * r:64 * r + 64], xtb[:, b, 64 * r:64 * r + 64], dec[:, r, col:col + 1], ALU.mult)
                    yield
            bst = cx.bank()
            P.mm(bst[:, 0:256], btok[:, 0, :], xw[:, 0, :], start=True, stop=False)
            yield
            P.mm(bst[:, 0:256], btok[:, 1, :], xw[:, 1, :], start=False, stop=True)
            yield
            for r in range(4):
                P.stt(st32[:, 64 * r:64 * r + 64], st32[:, 64 * r:64 * r + 64], E[:, r, 255:256], bst[:, 64 * r:64 * r + 64], ALU.mult, ALU.add)
                yield
            P.copy(stb[:], st32[:])
            yield


    wst = Rot([P.sb([128, KC, 256], F32, name=f"m_wst{i}") for i in range(2)])
    for g in range(8):
        w = wg.next()
        for (c0, n, d0) in [(256 * g, 256, 0), (2048 + 256 * g, 256, 256), (4096 + 128 * g, 128, 512),
                            (5120 + 128 * g, 128, 640), (6144 + 4 * g, 4, 768)]:
            st = wst.next()
            P.dma(st[:, :, 0:n], winv[:, :, c0:c0 + n])
            P.copy(w[:, :, d0:d0 + n], st[:, :, 0:n], eng='pool')
        chs = [2 * g, 2 * g + 1, 16 + g, 24 + g]
        gens = [chain(s, w, chs, g, BF[s]) for s in range(NS)]
        while gens:
            for g_ in list(gens):
                try:
                    next(g_)
                except StopIteration:
                    gens.remove(g_)


def phase_mamba(cx, xT, i_layer, j):
    nc = cx.P.nc
    phase_prenorm(cx, xT, cx.A['mix_norm_w'][i_layer], cx.hT)
    cx.P.finish()
    cx.P = Prog(nc)
    mamba_core(cx, j)
    cx.P.finish()
    cx.P = Prog(nc)
    phase_outproj(cx, xT, cx.yT, cx.A['mamba_w_out'][j], 2048)
import math as _math


def s5_prep(cx, j):
    P = cx.P
    A = cx.A
    NS, L, T = cx.NS, cx.L, cx.T
    NP = 32
    H = L // 2
    npass = int(round(_math.log2(L)))
    PI = _math.pi
    hTv = cx.hT.rearrange("(k p) t -> p k t", p=128)
    gTv = cx.yT.rearrange("(k p) t -> p k t", p=128)
    f = lambda n, shp: P.sb(shp, F32, name="s5_" + n)
    lr = f('lr', [128, NP]); li = f('li', [128, NP]); dt = f('dt', [128, NP])
    P.dma(lr[:], A['s5_lr'][j]); P.dma(li[:], A['s5_li'][j]); P.dma(dt[:], A['s5_ldt'][j])
    bre = f('bre', [128, NP, 16]); bim = f('bim', [128, NP, 16]); cre = f('cre', [128, NP, 16]); cim = f('cim', [128, NP, 16])
    P.dma(bre[:], A['s5_bre'][j]); P.dma(bim[:], A['s5_bim'][j]); P.dma(cre[:], A['s5_cre'][j]); P.dma(cim[:], A['s5_cim'][j])
    dcol = f('dcol', [128, KC]); P.dma(dcol[:], A['s5_dk'][j])
    P.act(dt[:], dt[:], AF.Exp)
    mag = f('mag', [128, NP]); th = f('th', [128, NP]); t1 = f('t1', [128, NP]); t2 = f('t2', [128, NP])
    P.tt(mag[:], lr[:], dt[:], ALU.mult)
    P.act(mag[:], mag[:], AF.Exp)
    P.tt(th[:], li[:], dt[:], ALU.mult)

    def sin_of(out, ang, shift):
        q = f('q%d' % P.nbuf, [128, NP]); qi = P.sb([128, NP], mybir.dt.int32, name='s5_qi%d' % P.nbuf); r = f('r%d' % P.nbuf, [128, NP]); m = f('m%d' % P.nbuf, [128, NP])
        P.ts(q[:], ang[:], shift, ALU.add, 1.0 / (2 * PI), ALU.mult)
        P.copy(qi[:], q[:])
        P.copy(q[:], qi[:])
        P.ts(r[:], ang[:], shift, ALU.add)
        P.stt(r[:], q[:], -2 * PI, r[:], ALU.mult, ALU.add)
        P.ts(m[:], r[:], PI, ALU.is_gt)
        P.stt(r[:], m[:], -2 * PI, r[:], ALU.mult, ALU.add)
        P.ts(m[:], r[:], -PI, ALU.is_lt)
        P.stt(r[:], m[:], 2 * PI, r[:], ALU.mult, ALU.add)
        P.ts(r[:], r[:], PI, ALU.min, -PI, ALU.max)
        P.act(out[:], r[:], AF.Sin)

    sn = f('sn', [128, NP]); cs = f('cs', [128, NP])
    sin_of(sn, th, 0.0)
    sin_of(cs, th, PI / 2)
    pw = f('pw', [128, npass, 3, NP])
    P.tt(pw[:, 0, 0, :], mag[:], cs[:], ALU.mult)
    P.tt(pw[:, 0, 1, :], mag[:], sn[:], ALU.mult)
    for k in range(1, npass):
        P.tt(t1[:], pw[:, k - 1, 0, :], pw[:, k - 1, 0, :], ALU.mult)
        P.tt(t2[:], pw[:, k - 1, 1, :], pw[:, k - 1, 1, :], ALU.mult)
        P.tt(pw[:, k, 0, :], t1[:], t2[:], ALU.subtract)
        P.tt(t1[:], pw[:, k - 1, 0, :], pw[:, k - 1, 1, :], ALU.mult)
        P.ts(pw[:, k, 1, :], t1[:], 2.0, ALU.mult)
    for k in range(npass):
        P.ts(pw[:, k, 2, :], pw[:, k, 1, :], -1.0, ALU.mult)
    den = f('den', [128, NP]); zr = f('zr', [128, NP]); zi = f('zi', [128, NP]); lm1 = f('lm1', [128, NP])
    P.tt(t1[:], lr[:], lr[:], ALU.mult)
    P.tt(t2[:], li[:], li[:], ALU.mult)
    P.tt(den[:], t1[:], t2[:], ALU.add)
    P.recip(den[:], den[:])
    P.ts(lm1[:], pw[:, 0, 0, :], -1.0, ALU.add)
    P.tt(t1[:], lm1[:], lr[:], ALU.mult)
    P.tt(t2[:], pw[:, 0, 1, :], li[:], ALU.mult)
    P.tt(zr[:], t1[:], t2[:], ALU.add)
    P.tt(zr[:], zr[:], den[:], ALU.mult)
    P.tt(t1[:], pw[:, 0, 1, :], lr[:], ALU.mult)
    P.tt(t2[:], lm1[:], li[:], ALU.mult)
    P.tt(zi[:], t1[:], t2[:], ALU.subtract)
    P.tt(zi[:], zi[:], den[:], ALU.mult)
    bbr = f('bbr', [128, NP, 16]); bbi = f('bbi', [128, NP, 16]); t3 = f('t3', [128, NP, 16])
    zrb = zr[:, :].unsqueeze(2).to_broadcast([128, NP, 16])
    zib = zi[:, :].unsqueeze(2).to_broadcast([128, NP, 16])
    P.tt(bbr[:], bre[:], zrb, ALU.mult)
    P.tt(t3[:], bim[:], zib, ALU.mult)
    P.tt(bbr[:], bbr[:], t3[:], ALU.subtract)
    P.tt(bbi[:], bim[:], zrb, ALU.mult)
    P.tt(t3[:], bre[:], zib, ALU.mult)
    P.tt(bbi[:], bbi[:], t3[:], ALU.add)
    P.ts(cim[:], cim[:], -1.0, ALU.mult)
    WB = P.sb([128, NP, 2, 128], BF16, name="s5_WB")
    WC = P.sb([128, NP, 2, 128], BF16, name="s5_WC")
    P.memset(WC[:], 0.0)
    zb = Rot([f('zb%d' % i, [128, 128]) for i in range(2)])
    for p in range(NP):
        pp = p % 4
        for ri, (bsrc, csrc) in enumerate([(bbr, cre), (bbi, cim)]):
            z = zb.next()
            P.memset(z[:], 0.0, eng='pool')
            P.copy(z[0:64, 32 * pp:32 * pp + 16], bsrc[0:64, p, :], eng='pool')
            P.copy(z[64:128, 32 * pp + 16:32 * pp + 32], bsrc[64:128, p, :], eng='pool')
            bk = cx.bank()
            P.transpose(bk[:, 0:128], z[:], cx.k32('ident'))
            P.copy(WB[:, p, ri, :], bk[:, 0:128], eng='act')
            P.copy(WC[0:64, p, ri, 32 * pp:32 * pp + 16], csrc[0:64, p, :])
            P.copy(WC[64:128, p, ri, 32 * pp + 16:32 * pp + 32], csrc[64:128, p, :])
    P.dma(cx.s5_WB, WB[:].rearrange("p a b c -> p (a b c)"))
    P.dma(cx.s5_WC, WC[:].rearrange("p a b c -> p (a b c)"))
    P.dma(cx.s5_pw[:, 0:npass * 3 * NP], pw[:].rearrange("p a b c -> p (a b c)"))
    P.dma(cx.s5_dcol, dcol[:])


def s5_core(cx, j):
    P = cx.P
    A = cx.A
    NS, L, T = cx.NS, cx.L, cx.T
    NP = 32
    H = L // 2
    npass = int(round(_math.log2(L)))
    hTv = cx.hT.rearrange("(k p) t -> p k t", p=128)
    gTv = cx.yT.rearrange("(k p) t -> p k t", p=128)
    WB = P.sb([128, NP, 2, 128], BF16, name="s5_WB")
    WC = P.sb([128, NP, 2, 128], BF16, name="s5_WC")
    pw = P.sb([128, npass, 3, NP], F32, name="s5_pw")
    dcol = P.sb([128, KC], F32, name="s5_dcol")
    P.dma(WB[:].rearrange("p a b c -> p (a b c)"), cx.s5_WB)
    P.dma(WC[:].rearrange("p a b c -> p (a b c)"), cx.s5_WC)
    P.dma(pw[:].rearrange("p a b c -> p (a b c)"), cx.s5_pw[:, 0:npass * 3 * NP])
    P.dma(dcol[:], cx.s5_dcol)
    W = H + L
    bufs = [[P.sb([128, W], F32, name=f"s5_s{a}{b}") for b in range(2)] for a in range(2)]
    for a in range(2):
        for b in range(2):
            P.memset(bufs[a][b][:, 0:H], 0.0)
    CT = min(L, 1024)
    tAr = Rot([P.sb([128, CT], F32, name=f"s5_tA{i}") for i in range(2)])
    tBr = Rot([P.sb([128, CT], F32, name=f"s5_tB{i}") for i in range(2)])
    fin = [P.sb([128, L], BF16, name=f"s5_fin{b}") for b in range(2)]
    yacc = P.sb([128, L], F32, name="s5_yacc")
    u = P.sb([128, L], BF16, name="s5_u")
    gel = Rot([P.sb([128, 512], F32, name=f"s5_g{i}") for i in range(4)])
    gout = Rot([P.sb([128, 512], BF16, name=f"s5_go{i}") for i in range(2)])
    NT = L // 512
    for kc in range(KC):
        for s in range(NS):
            P.dma(u[:], hTv[:, kc, s * L:(s + 1) * L])
            for pp in range(4):
                p = kc * 4 + pp
                cur = 0
                for ti in range(NT):
                    for ri in range(2):
                        bk = cx.bank()
                        P.mm(bk[:, 0:512], WB[:, p, ri, :], u[:, ti * 512:(ti + 1) * 512], start=True, stop=True)
                        P.copy(bufs[0][ri][:, H + ti * 512:H + (ti + 1) * 512], bk[:, 0:512], eng='act')
                for k in range(npass):
                    d = 1 << k
                    src = bufs[cur]
                    last = (k == npass - 1)
                    dst = fin if last else bufs[1 - cur]
                    pr = pw[:, k, 0, p:p + 1]
                    pi_ = pw[:, k, 1, p:p + 1]
                    npi = pw[:, k, 2, p:p + 1]
                    for c0 in range(0, L, CT):
                        o0 = c0 if last else H + c0
                        o_re = dst[0][:, o0:o0 + CT]
                        o_im = dst[1][:, o0:o0 + CT]
                        tA = tAr.next(); tB = tBr.next()
                        P.stt(tA[:], src[0][:, H + c0 - d:H + c0 - d + CT], pr, src[0][:, H + c0:H + c0 + CT], ALU.mult, ALU.add)
                        P.stt(o_re, src[1][:, H + c0 - d:H + c0 - d + CT], npi, tA[:], ALU.mult, ALU.add)
                        P.stt(tB[:], src[1][:, H + c0 - d:H + c0 - d + CT], pr, src[1][:, H + c0:H + c0 + CT], ALU.mult, ALU.add)
                        P.stt(o_im, src[0][:, H + c0 - d:H + c0 - d + CT], pi_, tB[:], ALU.mult, ALU.add)
                    cur = 1 - cur
                for ti in range(NT):
                    bk = cx.bank()
                    P.mm(bk[:, 0:512], WC[:, p, 0, :], fin[0][:, ti * 512:(ti + 1) * 512], start=True, stop=False)
                    P.mm(bk[:, 0:512], WC[:, p, 1, :], fin[1][:, ti * 512:(ti + 1) * 512], start=False, stop=True)
                    if pp == 0:
                        P.copy(yacc[:, ti * 512:(ti + 1) * 512], bk[:, 0:512], eng='act')
                    else:
                        P.tt(yacc[:, ti * 512:(ti + 1) * 512], yacc[:, ti * 512:(ti + 1) * 512], bk[:, 0:512], ALU.add, eng='dve')
            for ti in range(NT):
                sl = slice(ti * 512, (ti + 1) * 512)
                y = gel.next(); y2 = gel.next(); z = gel.next()
                P.stt(y[:], u[:, ti * 512:(ti + 1) * 512], dcol[:, kc:kc + 1], yacc[:, sl], ALU.mult, ALU.add)
                P.act(y2[:], y[:], AF.Square)
                P.ts(y2[:], y2[:], 0.044715, ALU.mult, 1.0, ALU.add)
                P.tt(z[:], y2[:], y[:], ALU.mult, eng='pool')
                P.act(z[:], z[:], AF.Sigmoid, scale=1.5957691216057308)
                go = gout.next()
                P.tt(go[:], y[:], z[:], ALU.mult, eng='pool')
                P.dma(gTv[:, kc, s * L + ti * 512:s * L + (ti + 1) * 512], go[:])


def s5_glu(cx, xT, j):
    P = cx.P
    A = cx.A
    T = cx.T
    TT = 512
    w = P.sb([128, KC, 2 * D], BF16, name="glu_w")
    load_w_bf16(cx, w, A['s5_w_glu'][j], D, 2 * D)
    bg = P.sb([128, 16], F32, name="glu_b")
    P.dma(bg[:], A['s5_bglu'][j])
    xt = Rot([P.sb([128, KC, TT], F32, name=f"glu_x{i}") for i in range(2)])
    gs = Rot([P.sb([128, KC, TT], BF16, name=f"glu_g{i}") for i in range(2)])
    sg = Rot([P.sb([128, TT], F32, name=f"glu_s{i}") for i in range(2)])
    oo = Rot([P.sb([128, TT], F32, name=f"glu_o{i}") for i in range(2)])
    xTv = xT.rearrange("(k p) t -> p k t", p=128)
    gTv = cx.yT.rearrange("(k p) t -> p k t", p=128)
    for ti in range(T // TT):
        t0 = ti * TT
        x = xt.next(); g = gs.next()
        P.dma(x[:], xTv[:, :, t0:t0 + TT])
        P.dma(g[:], gTv[:, 0:KC, t0:t0 + TT])
        for m in range(KC):
            ba = cx.bank(); bb = cx.bank()
            for k in range(KC):
                P.mm(ba[:, 0:TT], w[:, k, m * 128:(m + 1) * 128], g[:, k, :], start=(k == 0), stop=(k == KC - 1))
            for k in range(KC):
                P.mm(bb[:, 0:TT], w[:, k, D + m * 128:D + (m + 1) * 128], g[:, k, :], start=(k == 0), stop=(k == KC - 1))
            s_ = sg.next(); o = oo.next()
            P.act(s_[:], bb[:, 0:TT], AF.Sigmoid, bias=bg[:, 8 + m:9 + m])
            P.stt(o[:], ba[:, 0:TT], bg[:, m:m + 1], s_[:], ALU.add, ALU.mult)
            P.tt(x[:, m, :], x[:, m, :], o[:], ALU.add, eng='pool')
        P.dma(xTv[:, :, t0:t0 + TT], x[:])


def phase_s5(cx, xT, i_layer, j):
    nc = cx.P.nc
    phase_prenorm(cx, xT, cx.A['mix_norm_w'][i_layer], cx.hT)
    cx.P.finish()
    cx.P = Prog(nc)
    s5_prep(cx, j)
    cx.P.finish()
    cx.P = Prog(nc)
    s5_core(cx, j)
    cx.P.finish()
    cx.P = Prog(nc)
    s5_glu(cx, xT, j)
DEPTH = 4
W_SHAPES = {
    'mix_norm_w': (4, 128, 8), 'ffn_norm_w': (4, 128, 8), 'ffn_w_in': (4, 1024, 5632), 'ffn_w_out': (4, 2816, 1024),
    'final_norm_w': (128, 8),
}


def build(NS, L, plan, extra_shapes=None):
    nc = bass.Bass("TRN2", target_bir_lowering=False)
    P = Prog(nc)
    consts, offs = make_consts()
    T = NS * L
    A = {}
    shapes = dict(W_SHAPES)
    if extra_shapes:
        shapes.update(extra_shapes)
    for name, shp in shapes.items():
        A[name] = P.dram(name, shp, F32, kind="ExternalInput")
    x_in = P.dram("x", [NS, L, D], F32, kind="ExternalInput")
    c_in = P.dram("consts", list(consts.shape), F32, kind="ExternalInput")
    out = P.dram("out", [NS, L, D], F32, kind="ExternalOutput")
    xT = P.dram("xT", [D, T], F32)
    cx = Ctx(P, c_in, offs)
    cx.hT = P.dram("hT", [D, T], BF16)
    cx.yT = P.dram("yT", [2048, T], BF16)
    cx.s5_WB = P.dram("s5WB", [128, 32 * 2 * 128], BF16)
    cx.s5_WC = P.dram("s5WC", [128, 32 * 2 * 128], BF16)
    cx.s5_pw = P.dram("s5pw", [128, 16 * 3 * 32], F32)
    cx.s5_dcol = P.dram("s5dcol", [128, 8], F32)
    cx.A = A
    cx.NS, cx.L, cx.T = NS, L, T
    for ph in plan:
        if cx.P is None:
            cx.P = Prog(nc)
        if ph == 'load':
            phase_load(cx, x_in, xT, NS, L)
        elif ph == 'final':
            phase_final(cx, xT, A['final_norm_w'], out, NS, L)
        elif ph[0] == 'ffn':
            i = ph[1]
            phase_ffn(cx, xT, A['ffn_norm_w'][i], A['ffn_w_in'][i], A['ffn_w_out'][i], T)
        elif ph[0] == 'mamba':
            phase_mamba(cx, xT, ph[1], ph[2])
        elif ph[0] == 's5':
            phase_s5(cx, xT, ph[1], ph[2])
        elif ph[0] == 'gdn':
            phase_gdn(cx, xT, ph[1], ph[2])
        cx.P.finish()
        cx.P = None
    return nc, consts
def _pk(v, n):
    return np.ascontiguousarray(np.asarray(v, np.float32).reshape(n, 128).T)


def prep_weights(inp):
    o = {}
    f = lambda a: np.ascontiguousarray(np.asarray(a, np.float32))
    o['mix_norm_w'] = np.stack([_pk(v, 8) for v in inp['mix_norm_w']])
    o['ffn_norm_w'] = np.stack([_pk(v, 8) for v in inp['ffn_norm_w']])
    o['final_norm_w'] = _pk(inp['final_norm_w'], 8)
    o['ffn_w_in'] = f(inp['ffn_w_in'])
    o['ffn_w_out'] = f(inp['ffn_w_out'])
    if 'mamba_w_in' in inp:
        o['mamba_w_in'] = f(inp['mamba_w_in'])
        o['mamba_w_out'] = f(inp['mamba_w_out'])
        cw = np.asarray(inp['mamba_conv_w'], np.float32)
        o['mamba_convw'] = np.ascontiguousarray(cw.transpose(0, 2, 1).reshape(-1, 32, 128, 4).transpose(0, 2, 1, 3))
        o['mamba_convb'] = np.stack([_pk(v, 32) for v in inp['mamba_conv_b']])
        o['mamba_drep'] = np.stack([_pk(np.repeat(np.asarray(v, np.float32), 64), 16) for v in inp['mamba_d']])
        o['mamba_normw'] = np.stack([_pk(v, 16) for v in inp['mamba_norm_w']])
        o['mamba_dtb'] = np.stack([np.ascontiguousarray(np.asarray(v, np.float32).reshape(8, 4).T) for v in inp['mamba_dt_bias']])
        o['mamba_alog'] = np.stack([np.ascontiguousarray(np.asarray(v, np.float32).reshape(8, 4).T) for v in inp['mamba_a_log']])
    if 's5_lam_re' in inp:
        prep_s5(inp, o)
    if 'gdn_w_in' in inp:
        prep_gdn(inp, o)
    return o


def prep_s5(inp, o):
    f = lambda a: np.asarray(a, np.float32)
    n = len(inp['s5_lam_re'])
    o['s5_lr'] = np.stack([_pk(f(v).ravel(), 32) for v in inp['s5_lam_re']])
    o['s5_li'] = np.stack([_pk(f(v).ravel(), 32) for v in inp['s5_lam_im']])
    o['s5_ldt'] = np.stack([_pk(np.repeat(f(v), 64), 32) for v in inp['s5_log_dt']])
    for nm, src in [('s5_bre', 's5_b_re'), ('s5_bim', 's5_b_im')]:
        o[nm] = np.stack([np.ascontiguousarray(f(v).reshape(32, 128, 16).transpose(1, 0, 2)) for v in inp[src]])
    for nm, src in [('s5_cre', 's5_c_re'), ('s5_cim', 's5_c_im')]:
        o[nm] = np.stack([np.ascontiguousarray(f(v).reshape(32, 2, 16, 64).transpose(1, 3, 0, 2).reshape(128, 32, 16)) for v in inp[src]])
    o['s5_dk'] = np.stack([_pk(v, 8) for v in inp['s5_d']])
    o['s5_w_glu'] = np.ascontiguousarray(f(inp['s5_w_glu']))
    o['s5_bglu'] = np.stack([_pk(v, 16) for v in inp['s5_b_glu']])
    return o


def prep_gdn(inp, o):
    f = lambda a: np.asarray(a, np.float32)
    o['gdn_w_in'] = np.ascontiguousarray(f(inp['gdn_w_in']))
    o['gdn_w_out'] = np.ascontiguousarray(f(inp['gdn_w_out']))
    cw = f(inp['gdn_conv_w'])
    o['gdn_convw'] = np.ascontiguousarray(cw.transpose(0, 2, 1).reshape(-1, 32, 128, 4).transpose(0, 2, 1, 3))
    o['gdn_normw'] = np.stack([_pk(v, 2) for v in inp['gdn_norm_w']])
    o['gdn_dtb'] = np.ascontiguousarray(f(inp['gdn_dt_bias']).reshape(-1, 1, 8))
    o['gdn_alog'] = np.ascontiguousarray(f(inp['gdn_a_log']).reshape(-1, 1, 8))
    return o


PLAN = ['load', ('mamba', 0, 0), ('ffn', 0), ('s5', 1, 0), ('ffn', 1), ('gdn', 2, 0), ('ffn', 2),
        ('mamba', 3, 1), ('ffn', 3), 'final']


def kernel(**inputs):
    x = np.ascontiguousarray(np.asarray(inputs['x'], np.float32))
    W = prep_weights(inputs)
    B, L, _ = x.shape
    n_cores = 8
    NS = B // n_cores
    shapes = {k: tuple(v.shape) for k, v in W.items()}
    nc, consts = build(NS, L, PLAN, extra_shapes=shapes)
    in_maps = []
    for c in range(n_cores):
        m = dict(W)
        m['x'] = x[NS * c:NS * (c + 1)]
        m['consts'] = consts
        in_maps.append(m)
    res = run_bass_kernel_spmd(nc, in_maps, core_ids=list(range(n_cores)))
    return np.concatenate([np.asarray(r['out'], np.float32) for r in res.results], axis=0)
```
